# Optimizing a Trainium2 kernel written in Bass

```python
import jax
import jax.numpy as jnp
from jax import lax
import numpy as np

D_MODEL = 1024
BATCH = 2
SEQ = 8192
DEPTH = 1

GRID_W = 64
CTX_LEN = 256
N_HEADS = 8
QK_NOPE_DIM = 128
QK_ROPE_DIM = 64
QK_HEAD_DIM = QK_NOPE_DIM + QK_ROPE_DIM
V_HEAD_DIM = 128
Q_LORA_RANK = 384
KV_LORA_RANK = 256
CONV_DIM = D_MODEL
CONV_KSIZE = 3
D_FF = 256 * ((8 * D_MODEL + 3 * 256 - 1) // (3 * 256))
ROPE_AXIS_DIM = QK_ROPE_DIM // 2
ROPE_THETA = 10000.0
Q_BLOCK = 128
NORM_EPS = 1e-6
MOD_CHUNKS = 6
IN_SIZES = (CONV_DIM, CONV_DIM, CONV_DIM, Q_LORA_RANK, KV_LORA_RANK, QK_ROPE_DIM, D_MODEL, D_MODEL)
IN_SPLITS = tuple(int(v) for v in np.cumsum(IN_SIZES)[:-1])
D_IN = int(sum(IN_SIZES))

kernel_name = "hybrid_shortconv_mla_dit_block"


def rms_norm(t, g):
    tf = t.astype(jnp.float32)
    tf = tf * lax.rsqrt(jnp.mean(tf * tf, axis=-1, keepdims=True) + NORM_EPS)
    return (tf * g.astype(jnp.float32)).astype(t.dtype)


def modulate(t, shift, scale):
    return t * (1 + scale) + shift


def adaln_params(cond, w_mod, b_mod):
    return jnp.split(jax.nn.silu(cond) @ w_mod + b_mod, MOD_CHUNKS, axis=-1)


def axial_rope_tables(rows, dtype):
    row_pos, col_pos = jnp.meshgrid(jnp.arange(rows), jnp.arange(GRID_W), indexing="ij")
    half = ROPE_AXIS_DIM // 2
    freqs = ROPE_THETA ** (-jnp.arange(half, dtype=jnp.float32) / half)
    ang = jnp.concatenate([row_pos.reshape(-1, 1).astype(jnp.float32) * freqs,
                           col_pos.reshape(-1, 1).astype(jnp.float32) * freqs], axis=-1)
    return jnp.cos(ang).astype(dtype), jnp.sin(ang).astype(dtype)


def rope_2d(t, cos, sin):
    half = ROPE_AXIS_DIM // 2
    outs = []
    for a in range(2):
        seg = t[..., a * ROPE_AXIS_DIM:(a + 1) * ROPE_AXIS_DIM]
        ca = cos[:, a * half:(a + 1) * half]
        sa = sin[:, a * half:(a + 1) * half]
        x1, x2 = seg[..., :half], seg[..., half:]
        outs.append(x1 * ca - x2 * sa)
        outs.append(x1 * sa + x2 * ca)
    return jnp.concatenate(outs, axis=-1)


def rope_tail(t, rope):
    if rope is None:
        return t
    cos, sin = rope
    return jnp.concatenate([t[..., :QK_NOPE_DIM], rope_2d(t[..., QK_NOPE_DIM:], cos, sin)], axis=-1)


def depthwise_conv3(u, w, b):
    y = lax.conv_general_dilated(u, w[:, None, :], window_strides=(1,), padding="SAME",
                                 dimension_numbers=("NWC", "WIO", "NWC"),
                                 feature_group_count=u.shape[-1])
    return y + b


def short_conv_branch(bx, cx, xx, conv_w, conv_b, w_conv_out):
    return (bx * depthwise_conv3(cx * xx, conv_w, conv_b)) @ w_conv_out


def mla_queries(q_a, q_a_norm, w_q_b, q_norm, rope):
    b, s, _ = q_a.shape
    q = (rms_norm(q_a, q_a_norm) @ w_q_b).reshape(b, s, N_HEADS, QK_HEAD_DIM).transpose(0, 2, 1, 3)
    q = rms_norm(q, q_norm)
    return rope_tail(q, rope)


def mla_keys_values(kv_a, k_rope, kv_a_norm, w_kv_b, k_norm, rope):
    b, s, _ = kv_a.shape
    kv = (rms_norm(kv_a, kv_a_norm) @ w_kv_b).reshape(b, s, N_HEADS, QK_NOPE_DIM + V_HEAD_DIM)
    kv = kv.transpose(0, 2, 1, 3)
    k_nope, v = kv[..., :QK_NOPE_DIM], kv[..., QK_NOPE_DIM:]
    k_r = jnp.broadcast_to(k_rope[:, None], (b, N_HEADS, s, QK_ROPE_DIM))
    k = rms_norm(jnp.concatenate([k_nope, k_r], axis=-1), k_norm)
    return rope_tail(k, rope), v


def attend(q, k, v):
    s = jnp.einsum("bhqd,bhkd->bhqk", q, k).astype(jnp.float32) * (QK_HEAD_DIM ** -0.5)
    p = jax.nn.softmax(s, axis=-1).astype(v.dtype)
    return jnp.einsum("bhqk,bhkd->bhqd", p, v)


def latent_attention(q, k_lat, v_lat, k_ctx, v_ctx):
    b, h, s, dk = q.shape
    k_all = jnp.concatenate([k_ctx, k_lat], axis=2)
    v_all = jnp.concatenate([v_ctx, v_lat], axis=2)
    q_blocks = q.reshape(b, h, s // Q_BLOCK, Q_BLOCK, dk).transpose(2, 0, 1, 3, 4)
    out = lax.map(lambda qb: attend(qb, k_all, v_all), q_blocks)
    return out.transpose(1, 2, 0, 3, 4).reshape(b, h, s, V_HEAD_DIM)


def merge_heads(o):
    b, h, s, d = o.shape
    return o.transpose(0, 2, 1, 3).reshape(b, s, h * d)


def gated_merge(y_conv, attn, g_conv_pre, g_attn_pre, b_gate, w_attn_o, w_out):
    y_attn = merge_heads(attn) @ w_attn_o
    g_conv = jax.nn.sigmoid(g_conv_pre + b_gate[:D_MODEL])
    g_attn = jax.nn.sigmoid(g_attn_pre + b_gate[D_MODEL:])
    return (g_conv * y_conv + g_attn * y_attn) @ w_out


def swiglu(h, w_ffn_in, w_ffn_out):
    gate, up = jnp.split(h @ w_ffn_in, 2, axis=-1)
    return (jax.nn.silu(gate) * up) @ w_ffn_out


def setup_inputs(seed: int = 0) -> dict:
    key = jax.random.key(seed)
    ks = jax.random.split(key, 23)

    def normal(k, shape, scale):
        return jax.random.normal(k, shape, jnp.float32) * scale

    def gain(k, n):
        return 1.0 + normal(k, (DEPTH, n), 0.02)

    return {
        "x": normal(ks[0], (BATCH, SEQ, D_MODEL), 1.0),
        "c": normal(ks[1], (BATCH, D_MODEL), 1.0),
        "ctx": normal(ks[2], (BATCH, CTX_LEN, D_MODEL), 1.0),
        "c_ctx": normal(ks[3], (D_MODEL,), 1.0),
        "w_mod": normal(ks[4], (DEPTH, D_MODEL, MOD_CHUNKS * D_MODEL), 0.5 * D_MODEL ** -0.5),
        "b_mod": normal(ks[5], (DEPTH, MOD_CHUNKS * D_MODEL), 0.02),
        "norm_mix": gain(ks[6], D_MODEL),
        "norm_ffn": gain(ks[7], D_MODEL),
        "w_in": normal(ks[8], (DEPTH, D_MODEL, D_IN), D_MODEL ** -0.5),
        "b_gate": normal(ks[9], (DEPTH, 2 * D_MODEL), 0.02),
        "conv_w": normal(ks[10], (DEPTH, CONV_KSIZE, CONV_DIM), CONV_KSIZE ** -0.5),
        "conv_b": normal(ks[11], (DEPTH, CONV_DIM), 0.02),
        "w_conv_out": normal(ks[12], (DEPTH, CONV_DIM, D_MODEL), CONV_DIM ** -0.5),
        "q_a_norm": gain(ks[13], Q_LORA_RANK),
        "w_q_b": normal(ks[14], (DEPTH, Q_LORA_RANK, N_HEADS * QK_HEAD_DIM), Q_LORA_RANK ** -0.5),
        "kv_a_norm": gain(ks[15], KV_LORA_RANK),
        "w_kv_b": normal(ks[16], (DEPTH, KV_LORA_RANK, N_HEADS * (QK_NOPE_DIM + V_HEAD_DIM)), KV_LORA_RANK ** -0.5),
        "q_norm": gain(ks[17], QK_HEAD_DIM),
        "k_norm": gain(ks[18], QK_HEAD_DIM),
        "w_attn_o": normal(ks[19], (DEPTH, N_HEADS * V_HEAD_DIM, D_MODEL), (N_HEADS * V_HEAD_DIM) ** -0.5),
        "w_out": normal(ks[20], (DEPTH, D_MODEL, D_MODEL), D_MODEL ** -0.5),
        "w_ffn_in": normal(ks[21], (DEPTH, D_MODEL, 2 * D_FF), D_MODEL ** -0.5),
        "w_ffn_out": normal(ks[22], (DEPTH, D_FF, D_MODEL), D_FF ** -0.5),
    }


def reference(x, c, ctx, c_ctx, w_mod, b_mod, norm_mix, norm_ffn, w_in, b_gate, conv_w, conv_b,
              w_conv_out, q_a_norm, w_q_b, kv_a_norm, w_kv_b, q_norm, k_norm, w_attn_o, w_out,
              w_ffn_in, w_ffn_out):
    rows = x.shape[1] // GRID_W
    rope = axial_rope_tables(rows, x.dtype)
    for l in range(DEPTH):
        last = l == DEPTH - 1
        sh1, sc1, g1, sh2, sc2, g2 = [m[:, None, :] for m in adaln_params(c, w_mod[l], b_mod[l])]
        csh1, csc1, cg1, csh2, csc2, cg2 = adaln_params(c_ctx, w_mod[l], b_mod[l])

        hc = modulate(rms_norm(ctx, norm_mix[l]), csh1, csc1)
        cbx, ccx, cxx, cq_a, ckv_a, ck_rope, cgc, cga = jnp.split(hc @ w_in[l], IN_SPLITS, axis=-1)
        k_ctx, v_ctx = mla_keys_values(ckv_a, ck_rope, kv_a_norm[l], w_kv_b[l], k_norm[l], None)

        hx = modulate(rms_norm(x, norm_mix[l]), sh1, sc1)
        bx, cx, xx, q_a, kv_a, k_rope, gc, ga = jnp.split(hx @ w_in[l], IN_SPLITS, axis=-1)
        y_conv = short_conv_branch(bx, cx, xx, conv_w[l], conv_b[l], w_conv_out[l])
        q_lat = mla_queries(q_a, q_a_norm[l], w_q_b[l], q_norm[l], rope)
        k_lat, v_lat = mla_keys_values(kv_a, k_rope, kv_a_norm[l], w_kv_b[l], k_norm[l], rope)
        attn = latent_attention(q_lat, k_lat, v_lat, k_ctx, v_ctx)
        x_mid = x + g1 * gated_merge(y_conv, attn, gc, ga, b_gate[l], w_attn_o[l], w_out[l])

        hx2 = modulate(rms_norm(x_mid, norm_ffn[l]), sh2, sc2)
        x_new = x_mid + g2 * swiglu(hx2, w_ffn_in[l], w_ffn_out[l])

        if not last:
            cy_conv = short_conv_branch(cbx, ccx, cxx, conv_w[l], conv_b[l], w_conv_out[l])
            q_ctx = mla_queries(cq_a, q_a_norm[l], w_q_b[l], q_norm[l], None)
            cattn = attend(q_ctx, k_ctx, v_ctx)
            ctx_mid = ctx + cg1 * gated_merge(cy_conv, cattn, cgc, cga, b_gate[l], w_attn_o[l], w_out[l])
            hc2 = modulate(rms_norm(ctx_mid, norm_ffn[l]), csh2, csc2)
            ctx = ctx_mid + cg2 * swiglu(hc2, w_ffn_in[l], w_ffn_out[l])
        x = x_new
    return x
```

```python
import math
from contextlib import ExitStack

import numpy as np
import concourse.bass as bass
import concourse.mybir as mybir
from concourse.bass_utils import run_bass_kernel_spmd

F32 = mybir.dt.float32
BF16 = mybir.dt.bfloat16
AF = mybir.ActivationFunctionType
ALU = mybir.AluOpType

D = 1024
SEQ = 8192
CTX = 256
NTOK = SEQ + CTX
OWN = 2048
NH = 8
DFF = 2816
EPS = 1e-6
NKT = NTOK // 128
SM_SCALE = 192.0 ** -0.5

C_NMIX, C_NFFN, C_BMOD, C_CONVW, C_CONVB, C_BGATE = 0, 8, 16, 48, 72, 80
C_QAN, C_KVAN = 96, 99
C_QN_NOPE, C_QN_ROPE, C_QN_ROPESW, C_KN_NOPE, C_KN_ROPE, C_KN_ROPESW = 101, 102, 103, 104, 105, 106
C_MASK, C_FIDX, C_SIGN, C_POS = 107, 109, 110, 111
NCONST = 240

ROPE_PERM = np.concatenate([np.arange(16, 32), np.arange(0, 16), np.arange(48, 64), np.arange(32, 48)])


class _Op:
    __slots__ = ("eng", "need", "is_dma", "key", "val")


class Reg:
    __slots__ = ("name", "last_w", "readers", "excl")

    def __init__(self, name, excl=False):
        self.name = name
        self.last_w = None
        self.readers = {}
        self.excl = excl


class Sched:
    ENGS = ("pe", "act", "dve", "pool", "sp")

    def __init__(self, nc, es):
        self.nc = nc
        self.es = es
        self.eng = {"pe": nc.tensor, "act": nc.scalar, "dve": nc.vector, "pool": nc.gpsimd, "sp": nc.sync}
        self.sem = {e: es.enter_context(nc.semaphore("sem_" + e)) for e in self.ENGS}
        self.cnt = {e: 0 for e in self.ENGS}
        self.waited = {e: {} for e in self.ENGS}
        self.dsem = {}
        self.dcnt = {}
        self.out_keys = set()

    def _wait(self, eng, semname, sem, val):
        w = self.waited[eng]
        if w.get(semname, 0) >= val:
            return
        w[semname] = val
        self.eng[eng].wait_ge(sem, val)

    def _dep(self, eng, d, raw):
        if d is None:
            return
        if d.is_dma:
            self._wait(eng, "d:" + d.key, self.dsem[d.key], d.val)
            return
        if d.eng == eng:
            if eng == "pe" or not raw:
                return
        assert self.cnt[d.eng] >= d.need, "dependency on an op not yet covered by a signal"
        self._wait(eng, "e:" + d.eng, self.sem[d.eng], d.need)

    def _deps(self, eng, reads, writes):
        for r in reads:
            self._dep(eng, r.last_w, True)
            if r.excl:
                for e2, rd in r.readers.items():
                    if e2 != eng and not isinstance(rd, list):
                        self._dep(eng, rd, False)
        for w in writes:
            self._dep(eng, w.last_w, False)
            for rd in w.readers.values():
                if isinstance(rd, list):
                    for x in rd:
                        self._dep(eng, x, False)
                else:
                    self._dep(eng, rd, False)

    def _record(self, op, reads, writes):
        for r in reads:
            if op.is_dma:
                r.readers.setdefault("dma", []).append(op)
            else:
                r.readers[op.eng] = op
        for w in writes:
            w.last_w = op
            w.readers = {}

    def op(self, eng, fn, reads=(), writes=(), sig=True):
        self._deps(eng, reads, writes)
        ins = fn()
        o = _Op()
        o.eng = eng
        o.is_dma = False
        o.key = None
        o.val = 0
        if sig:
            self.cnt[eng] += 1
            ins.then_inc(self.sem[eng], 1)
            o.need = self.cnt[eng]
        else:
            o.need = self.cnt[eng] + 1
        self._record(o, reads, writes)
        return o

    def dma(self, queue, key, out, in_, reads=(), writes=(), is_out=False):
        if key not in self.dsem:
            self.dsem[key] = self.es.enter_context(self.nc.semaphore("dsem_" + key))
            self.dcnt[key] = 0
        self._deps(queue, reads, writes)
        ins = self.eng[queue].dma_start(out=out, in_=in_)
        self.dcnt[key] += 16
        ins.then_inc(self.dsem[key], 16)
        o = _Op()
        o.eng = queue
        o.is_dma = True
        o.key = key
        o.val = self.dcnt[key]
        o.need = 0
        self._record(o, reads, writes)
        if is_out:
            self.out_keys.add(key)
        return o

    def barrier(self):
        for e in self.ENGS:
            for e2 in self.ENGS:
                if e2 != e and self.cnt[e2] > 0:
                    self._wait(e, "e:" + e2, self.sem[e2], self.cnt[e2])
            for k, s in self.dsem.items():
                if self.dcnt[k] > 0:
                    self._wait(e, "d:" + k, s, self.dcnt[k])

    def finish(self):
        for k in sorted(self.out_keys):
            self._wait("sp", "d:" + k, self.dsem[k], self.dcnt[k])


class _Stop(Exception):
    pass


def build_nc(stop=None, dumps=()):
    nc = bass.Bass("TRN2", target_bir_lowering=False)
    try:
        _build(nc, stop, dumps)
    except _Stop:
        pass
    return nc


def _build(nc, stop, dumps):

    def din(name, shape):
        return nc.dram_tensor(name, list(shape), F32, kind="ExternalInput").ap()

    d_xall = din("xall", (NTOK, D))
    d_consts = din("consts", (128, NCONST))
    d_cT = din("cT", (128, 16))
    d_g12b = din("g12b", (128, 2 * D))
    d_wmod = din("w_mod", (D, 6 * D))
    d_win = din("w_in", (D, 5824))
    d_wkr = din("w_kr", (D, 256))
    d_wqb = din("w_qb", (384, NH * 384))
    d_wkvb = din("w_kvb", (256, NH * 256))
    d_wco = din("w_co", (D, D))
    d_wao = din("w_ao", (D, D))
    d_wout = din("w_out", (D, D))
    d_wfi = din("w_fi", (D, 2 * DFF))
    d_wfo = din("w_fo", (DFF, D))
    d_out = nc.dram_tensor("out", [OWN, D], F32, kind="ExternalOutput").ap()
    d_xmid = nc.dram_tensor("xmid", [OWN, D], F32, kind="Internal").ap()

    with ExitStack() as es:
        E = es.enter_context
        S = Sched(nc, es)
        pe, act, dve, pool = nc.tensor, nc.scalar, nc.vector, nc.gpsimd
        NAMES = {}

        def checkpoint(name):
            if stop != name:
                return
            S.barrier()
            for dn in dumps:
                ap = NAMES[dn]
                shp = list(ap.shape)
                dt_ = ap.dtype
                d = nc.dram_tensor("dbg_" + dn, shp, dt_, kind="ExternalOutput").ap()
                S.dma("sp", "dbg", d, ap, is_out=True)
            S.finish()
            raise _Stop()

        def sb(name, cols, dt=F32, parts=128):
            return E(nc.sbuf_tensor(name, [parts, cols], dt))

        cst = sb("cst", NCONST)
        G12 = sb("G12", 2 * D)
        cTs = sb("cTs", 16)
        scf = sb("scf", 16)
        scb = sb("scb", 16, BF16)
        pm = sb("pm", 64)
        A1 = sb("A1", 16)
        A2 = sb("A2", 16)
        identb = sb("identb", 128, BF16)
        onesb = sb("onesb", 128, BF16)
        oneshi = sb("oneshi", 128, BF16)
        onesf = sb("onesf", 128)
        epsc = sb("epsc", 2)
        frq = sb("frq", 4)
        ang = sb("ang", 128)
        angk = sb("angk", 128)
        COS = sb("COS", 128)
        SIN = sb("SIN", 128)
        gq2 = sb("gq2", 4)
        ssc = sb("ssc", 8)
        rr = [sb("rr%d" % i, 512)[:, :] for i in range(4)]
        tmpA = sb("tmpA", 512)[:, :]
        tmpB = sb("tmpB", 512)[:, :]
        sqs = [sb("sq%d" % i, 3 * 512, BF16)[:, :] for i in range(2)]
        BA = sb("BA", 152 * 512, BF16)
        FA = sb("FA", 4104)

        def bav(off_kb, cols, parts=None):
            o = int(off_kb * 512)
            v = BA[:, o:o + cols]
            return v

        R = {}

        def reg(name):
            if name not in R:
                R[name] = Reg(name)
            return R[name]

        pbank = [E(nc.psum_tensor("pb%d" % i, [128, 512], F32)) for i in range(8)]
        preg = [Reg("pb%d" % i, excl=True) for i in range(8)]

        def tpv(i):
            return pbank[i][:, :].bitcast(BF16)

        r_cst, r_g12b, r_cT = reg("cst"), reg("G12"), reg("cTs")
        S.dma("sp", "init0", cst[:, :], d_consts[:, :], writes=[r_cst])
        S.dma("sp", "init1", cTs[:, :], d_cT[:, :], writes=[r_cT])
        S.dma("sp", "init2", G12[:, :], d_g12b[:, :], writes=[r_g12b])

        r_misc = reg("misc")
        S.op("pool", lambda: pool.memset(tmpA[:, 0:128], 0.0), writes=[reg("tmpA")])
        S.op("pool", lambda: pool.affine_select(out=tmpA[:, 0:128], in_=tmpA[:, 0:128], pattern=[[-1, 128]],
                                                compare_op=ALU.not_equal, fill=1.0, base=0, channel_multiplier=1),
             reads=[reg("tmpA")], writes=[reg("tmpA")])
        S.op("dve", lambda: dve.tensor_copy(out=identb[:, :], in_=tmpA[:, 0:128]), reads=[reg("tmpA")], writes=[reg("identb")])
        S.op("dve", lambda: dve.memset(onesb[:, :], 1.0), writes=[reg("onesb")])
        S.op("dve", lambda: dve.memset(oneshi[:, :], 1.0), writes=[reg("oneshi")])
        S.op("dve", lambda: dve.memset(onesf[:, :], 1.0), writes=[reg("onesf")])
        S.op("dve", lambda: dve.memset(epsc[:, :], EPS), writes=[reg("epsc")])
        S.op("dve", lambda: dve.memset(oneshi[0:64, :], 0.0), writes=[reg("oneshi")])

        r_tab = reg("tab")
        S.op("act", lambda: act.activation(out=frq[:, 0:1], in_=cst[:, C_FIDX:C_FIDX + 1], func=AF.Exp,
                                           scale=-math.log(10000.0) / 16.0), reads=[r_cst], writes=[reg("frq")])
        TWO_PI = 2.0 * math.pi
        MAGIC = 12582912.0

        def sin_table(dst, shift, rname):
            S.op("dve", lambda: dve.tensor_scalar(out=ang[:, :], in0=cst[:, C_POS:C_POS + 128], scalar1=frq[:, 0:1],
                                                  scalar2=shift, op0=ALU.mult, op1=ALU.add),
                 reads=[r_cst, reg("frq")], writes=[reg("ang")])
            S.op("dve", lambda: dve.tensor_scalar(out=angk[:, :], in0=ang[:, :], scalar1=1.0 / TWO_PI, scalar2=MAGIC,
                                                  op0=ALU.mult, op1=ALU.add), reads=[reg("ang")], writes=[reg("angk")])
            S.op("dve", lambda: dve.tensor_scalar(out=angk[:, :], in0=angk[:, :], scalar1=MAGIC, scalar2=-TWO_PI,
                                                  op0=ALU.subtract, op1=ALU.mult), reads=[reg("angk")], writes=[reg("angk")])
            S.op("dve", lambda: dve.tensor_tensor(out=ang[:, :], in0=ang[:, :], in1=angk[:, :], op=ALU.add),
                 reads=[reg("ang"), reg("angk")], writes=[reg("ang")])
            S.op("dve", lambda: dve.tensor_scalar(out=ang[:, :], in0=ang[:, :], scalar1=-3.14159, scalar2=3.14159,
                                                  op0=ALU.max, op1=ALU.min), reads=[reg("ang")], writes=[reg("ang")])
            S.op("act", lambda: act.activation(out=dst[:, :], in_=ang[:, :], func=AF.Sin), reads=[reg("ang")], writes=[reg(rname)])

        sin_table(COS, math.pi / 2.0, "COS")
        sin_table(SIN, 0.0, "SINraw")
        S.op("dve", lambda: dve.tensor_scalar(out=SIN[:, :], in0=SIN[:, :], scalar1=cst[:, C_SIGN:C_SIGN + 1], scalar2=None,
                                              op0=ALU.mult), reads=[reg("SINraw"), r_cst], writes=[reg("SIN")])
        S.op("dve", lambda: dve.tensor_tensor(out=gq2[:, 0:1], in0=cst[:, C_QN_NOPE:C_QN_NOPE + 1],
                                              in1=cst[:, C_KN_NOPE:C_KN_NOPE + 1], op=ALU.mult), reads=[r_cst], writes=[reg("gq2")])

        S.op("act", lambda: act.activation(out=scf[:, :], in_=cTs[:, :], func=AF.Silu), reads=[r_cT], writes=[reg("scf")])
        S.op("dve", lambda: dve.tensor_copy(out=scb[:, :], in_=scf[:, :]), reads=[reg("scf")], writes=[reg("scb")])
        screp = sb("screp", 8 * 128, BF16)[:, :]
        r_screp = reg("screp")
        S.op("dve", lambda: dve.tensor_copy(out=screp.rearrange("p (k m) -> p k m", m=128),
                                            in_=bass.AP(scf, 0, [[16, 128], [2, 8], [0, 128]])),
             reads=[reg("scf")], writes=[r_screp])

        wm = [bav(113.5, 8 * 1024), bav(129.5, 8 * 1024)]
        r_wm = [reg("wm0"), reg("wm1")]
        r_pm = reg("pm")
        r_A = reg("A12")

        def wm_load(gi, s):
            S.dma("pool", "wm%d" % s, wm[s].rearrange("p (k n) -> p k n", k=8),
                  d_wmod[:, gi * 1024:(gi + 1) * 1024].rearrange("(k p) n -> p k n", p=128), writes=[r_wm[s]])

        def wm_fm(s, pi):
            for m in range(8):
                for k in range(8):
                    last = (m == 7 and k == 7)
                    S.op("pe", lambda m=m, k=k: pe.matmul(
                        pbank[0][:, (pi * 8 + m) * 2:(pi * 8 + m) * 2 + 2],
                        lhsT=wm[s][:, k * 1024 + m * 128:k * 1024 + (m + 1) * 128],
                        rhs=scb[:, k * 2:k * 2 + 2], start=(k == 0), stop=(k == 7)),
                        reads=[r_wm[s], reg("scb")], writes=[preg[0]], sig=last)

        def wm_tm(s, gg, extra_reads=()):
            for half in range(2):
                bk = 1 + half
                for k in range(8):
                    S.op("pe", lambda k=k: pe.matmul(
                        pbank[bk][:, :], lhsT=screp[:, k * 128:(k + 1) * 128],
                        rhs=wm[s][:, k * 1024 + half * 512:k * 1024 + (half + 1) * 512], start=(k == 0), stop=(k == 7)),
                        reads=[r_wm[s], r_screp] + list(extra_reads), writes=[preg[bk]], sig=(k == 7))
                o0 = gg * 1024 + half * 512
                S.op("dve", lambda: dve.tensor_tensor(out=G12[:, o0:o0 + 512], in0=pbank[bk][:, :],
                                                      in1=G12[:, o0:o0 + 512], op=ALU.add),
                     reads=[preg[bk], r_g12b], writes=[r_g12b])

        def pm_finish(lo):
            S.op("dve", lambda: dve.tensor_tensor(out=pm[:, lo:lo + 32].rearrange("p (a n) -> p a n", n=2),
                                                  in0=pbank[0][:, lo:lo + 32].rearrange("p (a n) -> p a n", n=2),
                                                  in1=bass.AP(cst, C_BMOD + lo // 2, [[NCONST, 128], [1, 16], [0, 2]]), op=ALU.add),
                 reads=[preg[0], r_cst], writes=[r_pm])

        def a_finish(Adst, col0, cn):
            S.op("dve", lambda: dve.tensor_scalar(out=Adst[:, :], in0=pm[:, col0:col0 + 16], scalar1=1.0, scalar2=None, op0=ALU.add),
                 reads=[r_pm], writes=[r_A])
            S.op("dve", lambda: dve.tensor_tensor(out=Adst[:, :].rearrange("p (a n) -> p a n", n=2),
                                                  in0=Adst[:, :].rearrange("p (a n) -> p a n", n=2),
                                                  in1=bass.AP(cst, cn, [[NCONST, 128], [1, 8], [0, 2]]), op=ALU.mult),
                 reads=[r_A, r_cst], writes=[r_A])

        wm_load(0, 0)
        wm_load(1, 1)
        wm_fm(0, 0)
        wm_fm(1, 1)
        pm_finish(0)
        a_finish(A1, 16, C_NMIX)
        wm_load(3, 0)
        wm_load(4, 1)
        NAMES.update(pm=pm[:, :], A1=A1[:, :], A2=A2[:, :], G12=G12[:, :], COS=COS[:, :], SIN=SIN[:, :], identb=identb[:, :])
        checkpoint("p0")

        xts = [FA[:, i * 1024:(i + 1) * 1024] for i in range(3)]
        xts += [bav(94 + 4 * i, 2048).bitcast(F32) for i in range(3)]
        r_xt = [reg("xt%d" % i) for i in range(6)]
        junk = bav(146, 1024)
        xnb = [bav(148, 1024), bav(150, 1024)]
        r_xn = [reg("xn0"), reg("xn1")]
        TPB = (6, 7)
        state = {"xt": 0, "xn": 0, "tp": 0, "ss": 0, "pb": 0}

        def norm_A(src_reg, src_ap):
            si = state["ss"] % 2
            state["ss"] += 1
            xi = state["xn"] % 2
            state["xn"] += 1
            r_ss = reg("ss%d" % si)
            c0 = si * 4
            S.op("act", lambda: act.activation(out=junk, in_=src_ap, func=AF.Square, accum_out=ssc[:, c0:c0 + 1]),
                 reads=[src_reg], writes=[reg("junk"), r_ss])
            S.op("act", lambda: act.activation(out=ssc[:, c0 + 1:c0 + 2], in_=ssc[:, c0:c0 + 1], func=AF.Ln,
                                               bias=epsc[:, 0:1], scale=1.0 / D), reads=[r_ss, reg("epsc")], writes=[r_ss])
            S.op("act", lambda: act.activation(out=ssc[:, c0 + 2:c0 + 3], in_=ssc[:, c0 + 1:c0 + 2], func=AF.Exp, scale=-0.5),
                 reads=[r_ss], writes=[r_ss])
            S.op("dve", lambda: dve.tensor_scalar(out=xnb[xi], in0=src_ap, scalar1=ssc[:, c0 + 2:c0 + 3], scalar2=None,
                                                  op0=ALU.mult), reads=[src_reg, r_ss], writes=[r_xn[xi]])
            return xi

        def norm_B(xi, Acol, Scol, dst_fn, dst_reg, ncols=128, force_act=False):
            ti = TPB[state["tp"] % 2]
            use_act = force_act or (state["tp"] % 3 == 0)
            state["tp"] += 1
            tv = tpv(ti)
            for j in range(8):
                S.op("pe", lambda j=j: pe.transpose(out=tv[:, j * 128:(j + 1) * 128], in_=xnb[xi][:, j * 128:(j + 1) * 128],
                                                    identity=identb[:, :]),
                     reads=[r_xn[xi], reg("identb")], writes=[preg[ti]], sig=(j == 7))
            for j in range(8):
                if use_act:
                    S.op("act", lambda j=j: act.activation(out=dst_fn(j), in_=tv[:, j * 128:j * 128 + ncols], func=AF.Identity,
                                                           bias=Scol(j), scale=Acol(j)),
                         reads=[preg[ti], r_A, r_pm], writes=[dst_reg])
                else:
                    S.op("dve", lambda j=j: dve.tensor_scalar(out=dst_fn(j), in0=tv[:, j * 128:j * 128 + ncols], scalar1=Acol(j),
                                                              scalar2=Scol(j), op0=ALU.mult, op1=ALU.add),
                         reads=[preg[ti], r_A, r_pm], writes=[dst_reg])

        def norm_transpose(src_reg, src_ap, Acol, Scol, dst_fn, dst_reg, ncols=128):
            xi = norm_A(src_reg, src_ap)
            norm_B(xi, Acol, Scol, dst_fn, dst_reg, ncols)

        def load_x(row0, nrows=128):
            i = state["xt"] % 6
            state["xt"] += 1
            S.dma("sp", "xt%d" % i, xts[i][0:nrows, :], d_xall[row0:row0 + nrows, :], writes=[r_xt[i]])
            return i

        def nextbank():
            b = state["pb"] % 6
            state["pb"] += 1
            return b

        kvnT = bav(0, 2 * NTOK)
        pack = bav(33, NTOK)
        qnT = bav(49.5, 3 * OWN)
        hxT = [bav(62, 8 * 512), bav(70, 8 * 512)]
        wA = bav(78, 8 * 704)
        wKR = bav(89, 8 * 256)
        CGk = FA[:, 3072:3584]
        SGk = FA[:, 3584:4096]
        r_hx = [reg("hxT0"), reg("hxT1")]
        r_wA, r_wKR = reg("wA"), reg("wKR")
        S.dma("pool", "wA", wA.rearrange("p (k n) -> p k n", k=8),
              d_win[:, 3072:3776].rearrange("(k p) n -> p k n", p=128), writes=[r_wA])
        S.dma("pool", "wKR", wKR.rearrange("p (k n) -> p k n", k=8),
              d_wkr[:, :].rearrange("(k p) n -> p k n", p=128), writes=[r_wKR])
        r_tk = reg("tabk")
        S.op("dve", lambda: dve.memset(FA[:, 3072:4096], 0.0), writes=[r_tk])
        S.op("dve", lambda: dve.tensor_scalar(out=CGk[32:64, :].rearrange("p (r c) -> p r c", c=64),
                                              in0=bass.AP(COS, 32 * 128, [[128, 32], [0, 8], [1, 64]]),
                                              scalar1=cst[32:64, C_KN_ROPE:C_KN_ROPE + 1], scalar2=None, op0=ALU.mult),
             reads=[reg("COS"), r_cst], writes=[r_tk])
        S.op("dve", lambda: dve.tensor_scalar(out=SGk[32:64, :].rearrange("p (r c) -> p r c", c=64),
                                              in0=bass.AP(SIN, 32 * 128, [[128, 32], [0, 8], [1, 64]]),
                                              scalar1=cst[32:64, C_KN_ROPESW:C_KN_ROPESW + 1], scalar2=None, op0=ALU.mult),
             reads=[reg("SIN"), r_cst], writes=[r_tk])

        def row_tables(Cg, Sg, g, c_cos, c_sin, rg):
            S.op("dve", lambda: dve.tensor_scalar(out=Cg[0:32, :].rearrange("p (r c) -> p r c", c=64),
                                                  in0=bass.AP(COS, g * 8, [[128, 32], [1, 8], [0, 64]]),
                                                  scalar1=cst[0:32, c_cos:c_cos + 1], scalar2=None, op0=ALU.mult),
                 reads=[reg("COS"), r_cst], writes=[rg])
            S.op("dve", lambda: dve.tensor_scalar(out=Sg[0:32, :].rearrange("p (r c) -> p r c", c=64),
                                                  in0=bass.AP(SIN, g * 8, [[128, 32], [1, 8], [0, 64]]),
                                                  scalar1=cst[0:32, c_sin:c_sin + 1], scalar2=None, op0=ALU.mult),
                 reads=[reg("SIN"), r_cst], writes=[rg])

        def rstd_bcast(stat_bank, scale, n):
            ri = state.setdefault("rr", 0) % 2
            state["rr"] = ri + 1
            a, b = rr[ri * 2], rr[ri * 2 + 1]
            ra, rb = reg("rr%d" % (ri * 2)), reg("rr%d" % (ri * 2 + 1))
            S.op("act", lambda: act.activation(out=a[:, 0:n], in_=pbank[stat_bank][:, 0:n], func=AF.Ln, bias=epsc[:, 0:1], scale=scale),
                 reads=[preg[stat_bank], reg("epsc")], writes=[ra])
            S.op("act", lambda: act.activation(out=b[:, 0:n], in_=a[:, 0:n], func=AF.Exp, scale=-0.5), reads=[ra], writes=[rb])
            return b, rb

        groups = [(g, g * 512, 512, False) for g in range(16)] + [(16, SEQ, 256, True)]
        slot_of = {}
        issued = [0]

        def ensure(upto):
            while issued[0] <= min(upto, NKT - 1):
                slot_of[issued[0]] = load_x(issued[0] * 128)
                issued[0] += 1

        def stage_A(T):
            ensure(T + 5)
            xi = slot_of[T]
            return norm_A(r_xt[xi], xts[xi])

        def post(gi):
            g, col0, n, is_ctx = groups[gi]
            own = g < 4
            hs = gi % 2
            def proj(bk, wt, rw, wcols, c_off):
                for k in range(8):
                    S.op("pe", lambda k=k: pe.matmul(pbank[bk][:, 0:n], lhsT=wt[:, k * wcols + c_off:k * wcols + c_off + 128],
                                                     rhs=hxT[hs][:, k * 512:k * 512 + n], start=(k == 0), stop=(k == 7)),
                         reads=[rw, r_hx[hs]], writes=[preg[bk]], sig=(k == 7))

            bkv = [nextbank(), nextbank()]
            for c in range(2):
                proj(bkv[c], wA, r_wA, 704, 384 + c * 128)
                yield
            bkr, bks = nextbank(), nextbank()
            proj(bkr, wKR, r_wKR, 256, 0)
            yield
            proj(bks, wKR, r_wKR, 256, 128)
            yield
            sq = sqs[gi % 2]
            r_sq = reg("sq%d" % (gi % 2))
            for c in range(2):
                S.op("act", lambda c=c: act.activation(out=sq[:, c * 512:c * 512 + n], in_=pbank[bkv[c]][:, 0:n], func=AF.Square),
                     reads=[preg[bkv[c]]], writes=[r_sq])
            bst = nextbank()
            for c in range(2):
                S.op("pe", lambda c=c: pe.matmul(pbank[bst][:, 0:n], lhsT=onesb[:, :], rhs=sq[:, c * 512:c * 512 + n],
                                                 start=(c == 0), stop=(c == 1)),
                     reads=[r_sq, reg("onesb")], writes=[preg[bst]], sig=(c == 1))
            rb, rrb = rstd_bcast(bst, 1.0 / 256.0, n)
            r_kvn = reg("kvn%d" % g)
            for c in range(2):
                S.op("dve", lambda c=c: dve.scalar_tensor_tensor(out=kvnT[:, c * NTOK + col0:c * NTOK + col0 + n],
                                                                 in0=pbank[bkv[c]][:, 0:n], scalar=cst[:, C_KVAN + c:C_KVAN + c + 1],
                                                                 in1=rb[:, 0:n], op0=ALU.mult, op1=ALU.mult),
                     reads=[preg[bkv[c]], rrb, r_cst], writes=[r_kvn])
            r_pk = reg("pack%d" % g)
            S.op("act", lambda: act.activation(out=pack[64:128, col0:col0 + n], in_=pbank[bkr][64:128, 0:n], func=AF.Square),
                 reads=[preg[bkr]], writes=[r_pk])
            if is_ctx:
                S.op("dve", lambda: dve.tensor_scalar(out=pack[0:64, col0:col0 + n], in0=pbank[bkr][0:64, 0:n],
                                                      scalar1=cst[0:64, C_KN_ROPE:C_KN_ROPE + 1], scalar2=None, op0=ALU.mult),
                     reads=[preg[bkr], r_cst], writes=[r_pk])
            else:
                row_tables(CGk, SGk, g, C_KN_ROPE, C_KN_ROPESW, r_tk)
                S.op("dve", lambda: dve.tensor_tensor(out=tmpA[0:64, 0:n], in0=pbank[bkr][0:64, 0:n], in1=CGk[0:64, 0:n], op=ALU.mult),
                     reads=[preg[bkr], r_tk], writes=[reg("tmpA")])
                S.op("dve", lambda: dve.tensor_tensor(out=tmpB[0:64, 0:n], in0=pbank[bks][0:64, 0:n], in1=SGk[0:64, 0:n], op=ALU.mult),
                     reads=[preg[bks], r_tk], writes=[reg("tmpB")])
                S.op("dve", lambda: dve.tensor_tensor(out=pack[0:64, col0:col0 + n], in0=tmpA[0:64, 0:n], in1=tmpB[0:64, 0:n], op=ALU.add),
                     reads=[reg("tmpA"), reg("tmpB")], writes=[r_pk])
            if own:
                bq = [nextbank(), nextbank(), nextbank()]
                for c in range(3):
                    proj(bq[c], wA, r_wA, 704, c * 128)
                    yield
                for c in range(3):
                    S.op("act", lambda c=c: act.activation(out=sq[:, c * 512:c * 512 + n], in_=pbank[bq[c]][:, 0:n], func=AF.Square),
                         reads=[preg[bq[c]]], writes=[r_sq])
                bst = nextbank()
                for c in range(3):
                    S.op("pe", lambda c=c: pe.matmul(pbank[bst][:, 0:n], lhsT=onesb[:, :], rhs=sq[:, c * 512:c * 512 + n],
                                                     start=(c == 0), stop=(c == 2)),
                         reads=[r_sq, reg("onesb")], writes=[preg[bst]], sig=(c == 2))
                rb, rrb = rstd_bcast(bst, 1.0 / 384.0, n)
                r_qn = reg("qn%d" % g)
                for c in range(3):
                    S.op("dve", lambda c=c: dve.scalar_tensor_tensor(out=qnT[:, c * OWN + col0:c * OWN + col0 + n],
                                                                     in0=pbank[bq[c]][:, 0:n], scalar=cst[:, C_QAN + c:C_QAN + c + 1],
                                                                     in1=rb[:, 0:n], op0=ALU.mult, op1=ALU.mult),
                         reads=[preg[bq[c]], rrb, r_cst], writes=[r_qn])

        hA = stage_A(0)
        pend = None
        for gi, (g, col0, n, is_ctx) in enumerate(groups):
            hs = gi % 2
            nn = 1 if is_ctx else 0
            for ti in range(n // 128):
                T = col0 // 128 + ti
                hN = stage_A(T + 1) if T + 1 < NKT else None
                norm_B(hA, lambda j: A1[:, j * 2 + nn:j * 2 + nn + 1], lambda j: pm[:, j * 2 + nn:j * 2 + nn + 1],
                       lambda j, ti=ti: hxT[hs][:, j * 512 + ti * 128:j * 512 + (ti + 1) * 128], r_hx[hs])
                hA = hN
                if pend is not None:
                    next(pend, None)
            if pend is not None:
                for _ in pend:
                    pass
            pend = post(gi)
        for _ in pend:
            pass
        wm_fm(0, 2)
        wm_fm(1, 3)
        pm_finish(32)
        a_finish(A2, 48, C_NFFN)
        S.barrier()
        wm_load(2, 0)
        wm_load(5, 1)
        NAMES.update(kvnT=kvnT, pack=pack, qnT=qnT)
        checkpoint("p1")

        KTn = bav(62, NTOK)
        Vh = bav(78.5, NTOK)
        QTn = bav(95, OWN)
        QTr = bav(99, OWN)
        PT = [bav(103 + i, 512) for i in range(4)]
        wkv = [bav(107, 2 * 256), bav(108, 2 * 256)]
        wq = [bav(109, 3 * 384), bav(111.25, 3 * 384)]
        attnT = bav(113.5, NH * OWN)
        CGq = FA[:, 0:512]
        SGq = FA[:, 512:1024]
        rl = FA[:, 1024:1536]
        sclk = FA[:, 1536:1536 + NKT]
        sclt = FA[:, 1664:1664 + NKT]
        r_KTn, r_Vh, r_QTn, r_QTr, r_attn = reg("KTn"), reg("Vh"), reg("QTn"), reg("QTr"), reg("attnT")
        r_PT = [reg("PT%d" % i) for i in range(4)]
        r_wkv = [reg("wkv0"), reg("wkv1")]
        r_wq = [reg("wq0"), reg("wq1")]
        r_tq = reg("tabq")
        r_scl = reg("sclk")
        S.op("dve", lambda: dve.memset(QTr, 0.0), writes=[r_QTr])
        S.op("dve", lambda: dve.memset(FA[:, 0:1024], 0.0), writes=[r_tq])
        S.op("dve", lambda: dve.tensor_scalar(out=CGq[32:64, :].rearrange("p (r c) -> p r c", c=64),
                                              in0=bass.AP(COS, 32 * 128, [[128, 32], [0, 8], [1, 64]]),
                                              scalar1=cst[32:64, C_QN_ROPE:C_QN_ROPE + 1], scalar2=None, op0=ALU.mult),
             reads=[reg("COS"), r_cst], writes=[r_tq])
        S.op("dve", lambda: dve.tensor_scalar(out=SGq[32:64, :].rearrange("p (r c) -> p r c", c=64),
                                              in0=bass.AP(SIN, 32 * 128, [[128, 32], [0, 8], [1, 64]]),
                                              scalar1=cst[32:64, C_QN_ROPESW:C_QN_ROPESW + 1], scalar2=None, op0=ALU.mult),
             reads=[reg("SIN"), r_cst], writes=[r_tq])

        def load_head_w(h):
            s = h % 2
            S.dma("pool", "wkv%d" % s, wkv[s].rearrange("p (c n) -> p c n", c=2),
                  d_wkvb[:, h * 256:(h + 1) * 256].rearrange("(c p) n -> p c n", p=128), writes=[r_wkv[s]])
            S.dma("pool", "wq%d" % s, wq[s].rearrange("p (c n) -> p c n", c=3),
                  d_wqb[:, h * 384:(h + 1) * 384].rearrange("(c p) n -> p c n", p=128), writes=[r_wq[s]])

        SB_, OB_, LB_, XB_ = (0, 1, 2, 5), (3, 4), 6, 7
        NPT = len(PT)
        accs = [FA[:, 2048:2560], FA[:, 2560:3072], FA[:, 3072:3584], FA[:, 3584:4096]]
        r_acc = [reg("acc%d" % i) for i in range(4)]
        load_head_w(0)
        it = 0
        for h in range(NH):
            s = h % 2
            if h + 1 < NH:
                load_head_w(h + 1)
            QB = ((0, 1, 2), (5, 3, 4))

            def q_mm(g):
                c0 = g * 512
                for (bk, off) in zip(QB[g % 2], (0, 128, 256)):
                    for c in range(3):
                        S.op("pe", lambda c=c, bk=bk, off=off: pe.matmul(
                            pbank[bk][:, :], lhsT=wq[s][:, c * 384 + off:c * 384 + off + 128],
                            rhs=qnT[:, c * OWN + c0:c * OWN + c0 + 512], start=(c == 0), stop=(c == 2)),
                            reads=[r_wq[s], reg("qn%d" % g)], writes=[preg[bk]], sig=(c == 2))

            def q_post(g):
                c0 = g * 512
                bn, br, bs = QB[g % 2]
                sq = sqs[g % 2]
                r_sq = reg("sq%d" % (g % 2))
                S.op("act", lambda: act.activation(out=sq[:, 0:512], in_=pbank[bn][:, :], func=AF.Square), reads=[preg[bn]], writes=[r_sq])
                S.op("act", lambda: act.activation(out=sq[:, 512:1024], in_=pbank[br][:, :], func=AF.Square), reads=[preg[br]], writes=[r_sq])
                for c in range(2):
                    S.op("pe", lambda c=c: pe.matmul(pbank[XB_][:, :], lhsT=onesb[:, :], rhs=sq[:, c * 512:(c + 1) * 512],
                                                     start=(c == 0), stop=(c == 1)),
                         reads=[r_sq, reg("onesb")], writes=[preg[XB_]], sig=(c == 1))
                rb, rrb = rstd_bcast(XB_, 1.0 / 192.0, 512)
                S.op("dve", lambda: dve.scalar_tensor_tensor(out=QTn[:, c0:c0 + 512], in0=pbank[bn][:, :], scalar=gq2[:, 0:1],
                                                             in1=rb[:, :], op0=ALU.mult, op1=ALU.mult),
                     reads=[preg[bn], rrb, reg("gq2")], writes=[r_QTn])
                row_tables(CGq, SGq, g, C_QN_ROPE, C_QN_ROPESW, r_tq)
                S.op("dve", lambda: dve.tensor_tensor(out=tmpA[0:64, :], in0=pbank[br][0:64, :], in1=CGq[0:64, :], op=ALU.mult),
                     reads=[preg[br], r_tq], writes=[reg("tmpA")])
                S.op("dve", lambda: dve.tensor_tensor(out=tmpB[0:64, :], in0=pbank[bs][0:64, :], in1=SGq[0:64, :], op=ALU.mult),
                     reads=[preg[bs], r_tq], writes=[reg("tmpB")])
                S.op("dve", lambda: dve.tensor_tensor(out=tmpA[0:64, :], in0=tmpA[0:64, :], in1=tmpB[0:64, :], op=ALU.add),
                     reads=[reg("tmpA"), reg("tmpB")], writes=[reg("tmpA")])
                S.op("dve", lambda: dve.tensor_tensor(out=QTr[0:64, c0:c0 + 512], in0=tmpA[0:64, :], in1=rb[0:64, :], op=ALU.mult),
                     reads=[reg("tmpA"), rrb], writes=[r_QTr])

            q_mm(0)
            for g in range(4):
                if g + 1 < 4:
                    q_mm(g + 1)
                q_post(g)
            NAMES.update(QTn=QTn, QTr=QTr, KTn=KTn, Vh=Vh, sclk=sclk, attnT=attnT)
            checkpoint("p2q")
            KB = (0, 1, 2)
            VB = (5, 3)

            def k_mm(gi):
                g, col0, n, is_ctx = groups[gi]
                bk = KB[gi % 3]
                for c in range(2):
                    S.op("pe", lambda c=c: pe.matmul(pbank[bk][:, 0:n], lhsT=wkv[s][:, c * 256:c * 256 + 128],
                                                     rhs=kvnT[:, c * NTOK + col0:c * NTOK + col0 + n], start=(c == 0), stop=(c == 1)),
                         reads=[r_wkv[s], reg("kvn%d" % g)], writes=[preg[bk]], sig=(c == 1))

            def k_post(gi):
                g, col0, n, is_ctx = groups[gi]
                bk = KB[gi % 3]
                sq = sqs[gi % 2]
                r_sq = reg("sq%d" % (gi % 2))
                S.op("act", lambda: act.activation(out=sq[:, 0:n], in_=pbank[bk][:, 0:n], func=AF.Square), reads=[preg[bk]], writes=[r_sq])
                S.op("act", lambda: act.activation(out=KTn[:, col0:col0 + n], in_=pbank[bk][:, 0:n], func=AF.Copy), reads=[preg[bk]], writes=[r_KTn])
                nt = n // 128
                bv = VB[gi % 2]
                for ti in range(nt):
                    kt = col0 // 128 + ti
                    for c in range(2):
                        S.op("pe", lambda ti=ti, kt=kt, c=c: pe.matmul(
                            pbank[bv][:, ti * 128:(ti + 1) * 128], lhsT=kvnT[:, c * NTOK + kt * 128:c * NTOK + (kt + 1) * 128],
                            rhs=wkv[s][:, c * 256 + 128:c * 256 + 256], start=(c == 0), stop=(c == 1)),
                            reads=[r_wkv[s], reg("kvn%d" % g)], writes=[preg[bv]], sig=(c == 1))
                for ti in range(nt):
                    kt = col0 // 128 + ti
                    S.op("pe", lambda ti=ti, kt=kt: pe.matmul(pbank[XB_][:, 2 * kt:2 * kt + 2], lhsT=sq[:, ti * 128:(ti + 1) * 128],
                                                              rhs=onesb[:, 0:2], start=True, stop=False),
                         reads=[r_sq, reg("onesb")], writes=[preg[XB_]], sig=False)
                    S.op("pe", lambda ti=ti, kt=kt: pe.matmul(pbank[XB_][:, 2 * kt:2 * kt + 2], lhsT=pack[:, kt * 128:(kt + 1) * 128],
                                                              rhs=oneshi[:, 0:2], start=False, stop=True),
                         reads=[reg("pack%d" % g), reg("oneshi")], writes=[preg[XB_]], sig=True)
                S.op("dve", lambda: dve.tensor_copy(out=Vh[:, col0:col0 + n], in_=pbank[bv][:, 0:n]), reads=[preg[bv]], writes=[r_Vh])

            k_mm(0)
            for gi in range(len(groups)):
                if gi + 1 < len(groups):
                    k_mm(gi + 1)
                k_post(gi)
            S.op("act", lambda: act.activation(out=sclt, in_=pbank[XB_][:, 0:2 * NKT:2], func=AF.Ln, bias=epsc[:, 0:1], scale=1.0 / 192.0),
                 reads=[preg[XB_], reg("epsc")], writes=[reg("sclt")])
            S.op("act", lambda: act.activation(out=sclk, in_=sclt, func=AF.Exp, scale=-0.5), reads=[reg("sclt")], writes=[r_scl])
            S.op("dve", lambda: dve.tensor_scalar(out=sclk, in0=sclk, scalar1=SM_SCALE, scalar2=None, op0=ALU.mult),
                 reads=[r_scl], writes=[r_scl])
            checkpoint("p2kv")
            if h == 0:
                wm_tm(0, 0, extra_reads=[r_attn])
                wm_tm(1, 1, extra_reads=[r_attn])
            for qt in range(4):
                q0 = qt * 512
                ob = OB_[it % 2]
                a0, a1 = (accs[0], accs[1]) if it % 2 == 0 else (accs[2], accs[3])
                ra0, ra1 = (r_acc[0], r_acc[1]) if it % 2 == 0 else (r_acc[2], r_acc[3])
                it += 1
                nd = 0

                def s_mm(i):
                    bk = SB_[i % 4]
                    S.op("pe", lambda: pe.matmul(pbank[bk][:, :], lhsT=KTn[:, i * 128:(i + 1) * 128], rhs=QTn[:, q0:q0 + 512],
                                                 start=True, stop=False),
                         reads=[r_KTn, r_QTn], writes=[preg[bk]], sig=False)
                    S.op("pe", lambda: pe.matmul(pbank[bk][:, :], lhsT=pack[:, i * 128:(i + 1) * 128], rhs=QTr[:, q0:q0 + 512],
                                                 start=False, stop=True),
                         reads=[reg("pack%d" % min(i // 4, 16)), r_QTr], writes=[preg[bk]], sig=True)

                LOOK = 3
                for i in range(LOOK):
                    s_mm(i)
                for i in range(NKT):
                    if i + LOOK < NKT:
                        s_mm(i + LOOK)
                    bk = SB_[i % 4]
                    p = i % NPT
                    S.op("act", lambda: act.activation(out=PT[p], in_=pbank[bk][:, :], func=AF.Exp, scale=sclk[:, i:i + 1]),
                         reads=[preg[bk], r_scl], writes=[r_PT[p]])
                    S.op("pe", lambda: pe.matmul(pbank[ob][:, :], lhsT=Vh[:, i * 128:(i + 1) * 128], rhs=PT[p],
                                                 start=(i == 0), stop=(i == NKT - 1)),
                         reads=[r_Vh, r_PT[p]], writes=[preg[ob]], sig=True)
                    if i % 16 == 15:
                        S.op("pe", lambda: pe.matmul(pbank[LB_][:, :], lhsT=onesb[:, :], rhs=PT[p], start=(i == 15), stop=False),
                             reads=[reg("onesb"), r_PT[p]], writes=[preg[LB_]], sig=True)
                    else:
                        aa, ra = (a0, ra0) if nd % 2 == 0 else (a1, ra1)
                        if nd < 2:
                            S.op("dve", lambda: dve.tensor_copy(out=aa, in_=PT[p]), reads=[r_PT[p]], writes=[ra])
                        else:
                            S.op("dve", lambda: dve.tensor_tensor(out=aa, in0=aa, in1=PT[p], op=ALU.add), reads=[ra, r_PT[p]], writes=[ra])
                        nd += 1
                S.op("dve", lambda: dve.tensor_tensor(out=a0, in0=a0, in1=a1, op=ALU.add), reads=[ra0, ra1], writes=[ra0])
                S.op("pe", lambda: pe.matmul(pbank[LB_][:, :], lhsT=onesf[:, :], rhs=a0, start=False, stop=True),
                     reads=[reg("onesf"), ra0], writes=[preg[LB_]], sig=True)
                S.op("act", lambda: act.activation(out=rl, in_=pbank[LB_][:, :], func=AF.Ln), reads=[preg[LB_]], writes=[reg("rl")])
                S.op("act", lambda: act.activation(out=rl, in_=rl, func=AF.Exp, scale=-1.0), reads=[reg("rl")], writes=[reg("rl")])
                S.op("dve", lambda: dve.tensor_tensor(out=attnT[:, h * OWN + q0:h * OWN + q0 + 512], in0=pbank[ob][:, :], in1=rl, op=ALU.mult),
                     reads=[preg[ob], reg("rl")], writes=[r_attn])
                checkpoint("p2a")
        S.barrier()
        NAMES.update(attnT=attnT, KTn=KTn, Vh=Vh, QTn=QTn, QTr=QTr)
        checkpoint("p2")

        HW_ = OWN + 2
        TMPS = [(tmpA, reg("tmpA")), (tmpB, reg("tmpB")), (rr[0], reg("rr0")), (rr[1], reg("rr1"))]
        hxTo = bav(0, 8 * HW_)
        vT = bav(33, 8 * OWN)
        mT = bav(65, 8 * OWN)
        wst = [bav(97, 8 * 512), bav(105, 8 * 512)]
        r_hxo, r_vT, r_mT = reg("hxTo"), reg("vT"), reg("mT")
        r_wst = [reg("wst0"), reg("wst1")]
        U = FA[:, 0:HW_]
        Y = FA[:, 2050:2050 + OWN]
        r_U, r_Y = reg("U"), reg("Y")
        S.op("dve", lambda: dve.memset(xts[2], 0.0), writes=[r_xt[2]])
        S.dma("sp", "xt2", xts[2][0:1, :], d_xall[SEQ - 1:SEQ, :], writes=[r_xt[2]])
        S.dma("sp", "xt2", xts[2][1:2, :], d_xall[OWN:OWN + 1, :], writes=[r_xt[2]])
        norm_transpose(r_xt[2], xts[2], lambda j: A1[:, j * 2:j * 2 + 1], lambda j: pm[:, j * 2:j * 2 + 1],
                       lambda j: hxTo[:, j * HW_:(j + 1) * HW_:HW_ - 1], r_hxo, ncols=2)
        S.dma("sp", "xt0", xts[0], d_xall[0:128, :], writes=[r_xt[0]])
        S.dma("sp", "xt1", xts[1], d_xall[128:256, :], writes=[r_xt[1]])
        hA = norm_A(r_xt[0], xts[0])
        for t in range(16):
            if t + 2 < 16:
                sl = (t + 2) % 3
                S.dma("sp", "xt%d" % sl, xts[sl], d_xall[(t + 2) * 128:(t + 3) * 128, :], writes=[r_xt[sl]])
            hN = norm_A(r_xt[(t + 1) % 3], xts[(t + 1) % 3]) if t + 1 < 16 else None
            norm_B(hA, lambda j: A1[:, j * 2:j * 2 + 1], lambda j: pm[:, j * 2:j * 2 + 1],
                   lambda j, t=t: hxTo[:, j * HW_ + 1 + t * 128:j * HW_ + 1 + (t + 1) * 128], r_hxo)
            hA = hN
        S.barrier()
        NAMES.update(hxTo=hxTo)
        checkpoint("p3i")
        pieces = [(0, 512), (512, 512), (1024, 512), (1536, 512), (2048, 2)]
        for j in range(8):
            s = j % 2
            for ci, cb in enumerate((0, 1024, 2048)):
                S.dma("pool", "wst%d" % s, wst[s][:, :].rearrange("p (k n) -> p k n", k=8)[:, :, ci * 128:(ci + 1) * 128],
                      d_win[:, cb + j * 128:cb + (j + 1) * 128].rearrange("(k p) n -> p k n", p=128), writes=[r_wst[s]])
            for (p0, pn) in pieces:
                bc, bx_ = nextbank(), nextbank()
                for (bk, off) in ((bc, 128), (bx_, 256)):
                    for k in range(8):
                        S.op("pe", lambda k=k, bk=bk, off=off: pe.matmul(
                            pbank[bk][:, 0:pn], lhsT=wst[s][:, k * 512 + off:k * 512 + off + 128],
                            rhs=hxTo[:, k * HW_ + p0:k * HW_ + p0 + pn], start=(k == 0), stop=(k == 7)),
                            reads=[r_wst[s], r_hxo], writes=[preg[bk]], sig=(k == 7))
                tsel = state.setdefault("tsel", 0)
                state["tsel"] = tsel + 1
                tX, rX = TMPS[tsel % 4]
                S.op("act", lambda: act.activation(out=tX[:, 0:pn], in_=pbank[bx_][:, 0:pn], func=AF.Copy),
                     reads=[preg[bx_]], writes=[rX])
                S.op("dve", lambda: dve.tensor_tensor(out=U[:, p0:p0 + pn], in0=pbank[bc][:, 0:pn], in1=tX[:, 0:pn], op=ALU.mult),
                     reads=[preg[bc], rX], writes=[r_U])
            S.op("dve", lambda: dve.tensor_scalar(out=U[:, 0:1], in0=U[:, 0:1], scalar1=cst[:, C_MASK:C_MASK + 1], scalar2=None,
                                                  op0=ALU.mult), reads=[r_U, r_cst], writes=[r_U])
            S.op("dve", lambda: dve.tensor_scalar(out=U[:, HW_ - 1:HW_], in0=U[:, HW_ - 1:HW_], scalar1=cst[:, C_MASK + 1:C_MASK + 2],
                                                  scalar2=None, op0=ALU.mult), reads=[r_U, r_cst], writes=[r_U])
            S.op("dve", lambda: dve.tensor_scalar(out=Y, in0=U[:, 0:OWN], scalar1=cst[:, C_CONVW + j:C_CONVW + j + 1],
                                                  scalar2=cst[:, C_CONVB + j:C_CONVB + j + 1], op0=ALU.mult, op1=ALU.add),
                 reads=[r_U, r_cst], writes=[r_Y])
            for kk in (1, 2):
                S.op("dve", lambda kk=kk: dve.scalar_tensor_tensor(out=Y, in0=U[:, kk:kk + OWN],
                                                                   scalar=cst[:, C_CONVW + kk * 8 + j:C_CONVW + kk * 8 + j + 1],
                                                                   in1=Y, op0=ALU.mult, op1=ALU.add),
                     reads=[r_U, r_Y, r_cst], writes=[r_Y])
            for q in range(4):
                bb = nextbank()
                for k in range(8):
                    S.op("pe", lambda k=k: pe.matmul(pbank[bb][:, :], lhsT=wst[s][:, k * 512:k * 512 + 128],
                                                     rhs=hxTo[:, k * HW_ + 1 + q * 512:k * HW_ + 1 + (q + 1) * 512],
                                                     start=(k == 0), stop=(k == 7)),
                         reads=[r_wst[s], r_hxo], writes=[preg[bb]], sig=(k == 7))
                S.op("dve", lambda: dve.tensor_tensor(out=vT[:, j * OWN + q * 512:j * OWN + (q + 1) * 512], in0=pbank[bb][:, :],
                                                      in1=Y[:, q * 512:(q + 1) * 512], op=ALU.mult),
                     reads=[preg[bb], r_Y], writes=[r_vT])
        for j in range(8):
            s = j % 2
            srcs = (d_win[:, 3776 + j * 128:3776 + (j + 1) * 128], d_win[:, 4800 + j * 128:4800 + (j + 1) * 128],
                    d_wco[:, j * 128:(j + 1) * 128], d_wao[:, j * 128:(j + 1) * 128])
            for ci, src in enumerate(srcs):
                S.dma("pool", "wst%d" % s, wst[s][:, :].rearrange("p (k n) -> p k n", k=8)[:, :, ci * 128:(ci + 1) * 128],
                      src.rearrange("(k p) n -> p k n", p=128), writes=[r_wst[s]])
            for q in range(4):
                c0 = q * 512
                bks = [nextbank() for _ in range(4)]
                rhs_of = (lambda k: hxTo[:, k * HW_ + 1 + c0:k * HW_ + 1 + c0 + 512],
                          lambda k: hxTo[:, k * HW_ + 1 + c0:k * HW_ + 1 + c0 + 512],
                          lambda k: vT[:, k * OWN + c0:k * OWN + c0 + 512],
                          lambda k: attnT[:, k * OWN + c0:k * OWN + c0 + 512])
                rregs = (r_hxo, r_hxo, r_vT, r_attn)
                for ci in range(4):
                    for k in range(8):
                        S.op("pe", lambda k=k, ci=ci: pe.matmul(pbank[bks[ci]][:, :],
                                                                lhsT=wst[s][:, k * 512 + ci * 128:k * 512 + (ci + 1) * 128],
                                                                rhs=rhs_of[ci](k), start=(k == 0), stop=(k == 7)),
                             reads=[r_wst[s], rregs[ci]], writes=[preg[bks[ci]]], sig=(k == 7))
                tsel = state.setdefault("tsel", 0)
                state["tsel"] = tsel + 1
                (tP, rP), (tQ, rQ) = TMPS[(2 * tsel) % 4], TMPS[(2 * tsel + 1) % 4]
                S.op("act", lambda: act.activation(out=tP, in_=pbank[bks[0]][:, :], func=AF.Sigmoid,
                                                   bias=cst[:, C_BGATE + j:C_BGATE + j + 1]),
                     reads=[preg[bks[0]], r_cst], writes=[rP])
                S.op("act", lambda: act.activation(out=tQ, in_=pbank[bks[1]][:, :], func=AF.Sigmoid,
                                                   bias=cst[:, C_BGATE + 8 + j:C_BGATE + 8 + j + 1]),
                     reads=[preg[bks[1]], r_cst], writes=[rQ])
                S.op("dve", lambda: dve.tensor_tensor(out=tP, in0=pbank[bks[2]][:, :], in1=tP, op=ALU.mult),
                     reads=[preg[bks[2]], rP], writes=[rP])
                S.op("dve", lambda: dve.tensor_tensor(out=tQ, in0=pbank[bks[3]][:, :], in1=tQ, op=ALU.mult),
                     reads=[preg[bks[3]], rQ], writes=[rQ])
                S.op("dve", lambda: dve.tensor_tensor(out=mT[:, j * OWN + c0:j * OWN + c0 + 512], in0=tP, in1=tQ, op=ALU.add),
                     reads=[rP, rQ], writes=[r_mT])
        S.barrier()
        NAMES.update(vT=vT, mT=mT)
        checkpoint("p3iii")
        wout = bav(0, 8 * D)
        hx2T = bav(16, 8 * OWN)
        r_wout, r_hx2 = reg("wout"), reg("hx2T")
        S.dma("pool", "wout", wout.rearrange("p (k n) -> p k n", k=8), d_wout[:, :].rearrange("(k p) n -> p k n", p=128), writes=[r_wout])
        xm = [FA[:, 2048:3072], FA[:, 3072:4096]]
        r_xm = [reg("xm0"), reg("xm1")]
        S.dma("sp", "xt0", xts[0], d_xall[0:128, :], writes=[r_xt[0]])

        def wout_mm(t):
            bo = [nextbank(), nextbank()]
            for half in range(2):
                for k in range(8):
                    S.op("pe", lambda k=k, half=half: pe.matmul(pbank[bo[half]][:, :], lhsT=mT[:, k * OWN + t * 128:k * OWN + (t + 1) * 128],
                                                                rhs=wout[:, k * D + half * 512:k * D + (half + 1) * 512],
                                                                start=(k == 0), stop=(k == 7)),
                         reads=[r_mT, r_wout], writes=[preg[bo[half]]], sig=(k == 7))
            return bo

        def resid(t, bo):
            xi = t % 2
            mi = t % 2
            for half in range(2):
                S.op("dve", lambda half=half: dve.tensor_tensor(out=xm[mi][:, half * 512:(half + 1) * 512], in0=pbank[bo[half]][:, :],
                                                                in1=G12[:, half * 512:(half + 1) * 512], op=ALU.mult),
                     reads=[preg[bo[half]], r_g12b], writes=[r_xm[mi]])
            S.op("dve", lambda: dve.tensor_tensor(out=xm[mi], in0=xm[mi], in1=xts[xi], op=ALU.add),
                 reads=[r_xm[mi], r_xt[xi]], writes=[r_xm[mi]])
            S.dma("sp", "xmo%d" % mi, d_xmid[t * 128:(t + 1) * 128, :], xm[mi], reads=[r_xm[mi]], writes=[reg("dxm%d" % t)])
            return norm_A(r_xm[mi], xm[mi])

        bo_cur = wout_mm(0)
        hprev = None
        for t in range(16):
            if t + 1 < 16:
                S.dma("sp", "xt%d" % ((t + 1) % 2), xts[(t + 1) % 2], d_xall[(t + 1) * 128:(t + 2) * 128, :], writes=[r_xt[(t + 1) % 2]])
            bo_next = wout_mm(t + 1) if t + 1 < 16 else None
            hcur = resid(t, bo_cur)
            if hprev is not None:
                norm_B(hprev[0], lambda j: A2[:, j * 2:j * 2 + 1], lambda j: pm[:, 32 + j * 2:32 + j * 2 + 1],
                       lambda j, tp_=hprev[1]: hx2T[:, j * OWN + tp_ * 128:j * OWN + (tp_ + 1) * 128], r_hx2, force_act=True)
            hprev = (hcur, t)
            bo_cur = bo_next
        norm_B(hprev[0], lambda j: A2[:, j * 2:j * 2 + 1], lambda j: pm[:, 32 + j * 2:32 + j * 2 + 1],
               lambda j, tp_=hprev[1]: hx2T[:, j * OWN + tp_ * 128:j * OWN + (tp_ + 1) * 128], r_hx2, force_act=True)
        S.barrier()
        NAMES.update(hx2T=hx2T)
        checkpoint("p3iv")

        HALF = OWN // 2
        actT = bav(48, 22 * HALF)
        wfo = bav(92, 22 * D)
        wfi = [bav(136 + 4 * i, 8 * 256) for i in range(4)]
        r_actT, r_wfo = reg("actT"), reg("wfo")
        r_wfi = [reg("wfi%d" % i) for i in range(4)]
        S.dma("pool", "wfo", wfo.rearrange("p (k n) -> p k n", k=22), d_wfo[:, :].rearrange("(k p) n -> p k n", p=128), writes=[r_wfo])
        res = [FA[:, 2048:3072], FA[:, 3072:4096]]
        r_res = [reg("res0"), reg("res1")]
        wcnt = 0
        for hf in range(2):
            t0 = hf * HALF
            for jj in range(22):
                s = wcnt % 4
                wcnt += 1
                for ci, cb in enumerate((0, DFF)):
                    S.dma("pool", "wfi%d" % s, wfi[s][:, :].rearrange("p (k n) -> p k n", k=8)[:, :, ci * 128:(ci + 1) * 128],
                          d_wfi[:, cb + jj * 128:cb + (jj + 1) * 128].rearrange("(k p) n -> p k n", p=128), writes=[r_wfi[s]])
                for q in range(2):
                    c0 = t0 + q * 512
                    bg, bu = nextbank(), nextbank()
                    for (bk, off) in ((bg, 0), (bu, 128)):
                        for k in range(8):
                            S.op("pe", lambda k=k, bk=bk, off=off: pe.matmul(
                                pbank[bk][:, :], lhsT=wfi[s][:, k * 256 + off:k * 256 + off + 128],
                                rhs=hx2T[:, k * OWN + c0:k * OWN + c0 + 512], start=(k == 0), stop=(k == 7)),
                                reads=[r_wfi[s], r_hx2], writes=[preg[bk]], sig=(k == 7))
                    tsel = state.setdefault("tsel", 0)
                    state["tsel"] = tsel + 1
                    tX, rX = TMPS[tsel % 4]
                    S.op("act", lambda: act.activation(out=tX, in_=pbank[bg][:, :], func=AF.Silu), reads=[preg[bg]], writes=[rX])
                    S.op("dve", lambda: dve.tensor_tensor(out=actT[:, jj * HALF + q * 512:jj * HALF + (q + 1) * 512], in0=pbank[bu][:, :],
                                                          in1=tX, op=ALU.mult),
                         reads=[preg[bu], rX], writes=[r_actT])
            for tt in range(8):
                t = hf * 8 + tt
                xi = t % 2
                S.dma("sp", "xt%d" % xi, xts[xi], d_xmid[t * 128:(t + 1) * 128, :], reads=[reg("dxm%d" % t)], writes=[r_xt[xi]])
                bo = [nextbank(), nextbank()]
                for half in range(2):
                    for k in range(22):
                        S.op("pe", lambda k=k, half=half: pe.matmul(pbank[bo[half]][:, :],
                                                                    lhsT=actT[:, k * HALF + tt * 128:k * HALF + (tt + 1) * 128],
                                                                    rhs=wfo[:, k * D + half * 512:k * D + (half + 1) * 512],
                                                                    start=(k == 0), stop=(k == 21)),
                             reads=[r_actT, r_wfo], writes=[preg[bo[half]]], sig=(k == 21))
                ri = t % 2
                for half in range(2):
                    S.op("dve", lambda half=half: dve.tensor_tensor(out=res[ri][:, half * 512:(half + 1) * 512], in0=pbank[bo[half]][:, :],
                                                                    in1=G12[:, D + half * 512:D + (half + 1) * 512], op=ALU.mult),
                         reads=[preg[bo[half]], r_g12b], writes=[r_res[ri]])
                S.op("dve", lambda: dve.tensor_tensor(out=res[ri], in0=res[ri], in1=xts[xi], op=ALU.add),
                     reads=[r_res[ri], r_xt[xi]], writes=[r_res[ri]])
                S.dma("sp", "out%d" % ri, d_out[t * 128:(t + 1) * 128, :], res[ri], reads=[r_res[ri]], writes=[reg("dout%d" % t)], is_out=True)
        S.finish()


def _fm(v, nch):
    return np.ascontiguousarray(np.asarray(v, np.float32).reshape(nch, 128).T)


def _prep_shared(inp):
    f = lambda k: np.asarray(inp[k], np.float32)
    w_in = f("w_in")[0]
    w_qb = f("w_q_b")[0]
    base = np.zeros((128, NCONST), np.float32)
    base[:, C_NMIX:C_NMIX + 8] = _fm(f("norm_mix")[0], 8)
    base[:, C_NFFN:C_NFFN + 8] = _fm(f("norm_ffn")[0], 8)
    bm = f("b_mod")[0]
    for pi, ch in enumerate((0, 1, 3, 4)):
        base[:, C_BMOD + pi * 8:C_BMOD + (pi + 1) * 8] = _fm(bm[ch * D:(ch + 1) * D], 8)
    cw = f("conv_w")[0]
    for k in range(3):
        base[:, C_CONVW + k * 8:C_CONVW + (k + 1) * 8] = _fm(cw[k], 8)
    base[:, C_CONVB:C_CONVB + 8] = _fm(f("conv_b")[0], 8)
    base[:, C_BGATE:C_BGATE + 16] = _fm(f("b_gate")[0], 16)
    base[:, C_QAN:C_QAN + 3] = _fm(f("q_a_norm")[0], 3)
    base[:, C_KVAN:C_KVAN + 2] = _fm(f("kv_a_norm")[0], 2)
    qn, kn = f("q_norm")[0], f("k_norm")[0]
    base[:, C_QN_NOPE] = qn[:128]
    base[:64, C_QN_ROPE] = qn[128:]
    base[:64, C_QN_ROPESW] = qn[128:][ROPE_PERM]
    base[:, C_KN_NOPE] = kn[:128]
    base[:64, C_KN_ROPE] = kn[128:]
    base[:64, C_KN_ROPESW] = kn[128:][ROPE_PERM]
    p = np.arange(64)
    base[:64, C_FIDX] = (p % 16).astype(np.float32)
    base[:, C_SIGN] = 1.0
    base[:64, C_SIGN] = np.where((p % 32) < 16, -1.0, 1.0)
    base[32:64, C_POS:C_POS + 128] = np.arange(128, dtype=np.float32)[None, :]
    kr = w_in[:, 3712:3776]
    w_kr = np.concatenate([kr, kr, kr[:, ROPE_PERM], np.zeros_like(kr)], axis=1)
    wq = w_qb.reshape(384, NH, 192)
    z = np.zeros((384, NH, 64), np.float32)
    w_qb_ext = np.concatenate([wq[:, :, :128], wq[:, :, 128:], z, wq[:, :, 128:][:, :, ROPE_PERM], z], axis=2).reshape(384, NH * 384)
    shared = {
        "w_mod": np.ascontiguousarray(f("w_mod")[0]),
        "w_in": np.ascontiguousarray(w_in),
        "w_kr": np.ascontiguousarray(w_kr),
        "w_qb": np.ascontiguousarray(w_qb_ext),
        "w_kvb": np.ascontiguousarray(f("w_kv_b")[0]),
        "w_co": np.ascontiguousarray(f("w_conv_out")[0]),
        "w_ao": np.ascontiguousarray(f("w_attn_o")[0]),
        "w_out": np.ascontiguousarray(f("w_out")[0]),
        "w_fi": np.ascontiguousarray(f("w_ffn_in")[0]),
        "w_fo": np.ascontiguousarray(f("w_ffn_out")[0]),
        "g12b": np.ascontiguousarray(np.broadcast_to(np.concatenate([bm[2 * D:3 * D], bm[5 * D:6 * D]])[None, :], (128, 2 * D))),
    }
    return base, shared


_NC_CACHE = {}


def make_in_maps(inputs):
    x = np.asarray(inputs["x"], np.float32)
    c = np.asarray(inputs["c"], np.float32)
    ctx = np.asarray(inputs["ctx"], np.float32)
    c_ctx = np.asarray(inputs["c_ctx"], np.float32)
    base, shared = _prep_shared(inputs)
    in_maps = []
    for core in range(8):
        b, qc = core // 4, core % 4
        xall = np.concatenate([np.roll(x[b], -qc * OWN, axis=0), ctx[b]], axis=0)
        cons = base.copy()
        cons[:, C_MASK] = 1.0 if qc > 0 else 0.0
        cons[:, C_MASK + 1] = 1.0 if qc < 3 else 0.0
        cons[:32, C_POS:C_POS + 128] = ((np.arange(128) + 32 * qc) % 128).astype(np.float32)[None, :]
        cv = np.stack([c[b], c_ctx], axis=1)
        cT = np.ascontiguousarray(cv.reshape(8, 128, 2).transpose(1, 0, 2).reshape(128, 16))
        m = {"xall": np.ascontiguousarray(xall), "consts": cons, "cT": cT}
        m.update(shared)
        in_maps.append(m)
    return in_maps


def kernel(**inputs):
    in_maps = make_in_maps(inputs)
    if "nc" not in _NC_CACHE:
        _NC_CACHE["nc"] = build_nc()
    nc = _NC_CACHE["nc"]
    res = run_bass_kernel_spmd(nc, in_maps, core_ids=list(range(8)))
    out = np.empty((2, SEQ, D), np.float32)
    for core in range(8):
        b, qc = core // 4, core % 4
        out[b, qc * OWN:(qc + 1) * OWN] = res.results[core]["out"]
    return out
```

```python
import math
from contextlib import ExitStack

import numpy as np
import concourse.bass as bass
import concourse.mybir as mybir
from concourse.bass_utils import run_bass_kernel_spmd

F32 = mybir.dt.float32
BF16 = mybir.dt.bfloat16
AF = mybir.ActivationFunctionType
ALU = mybir.AluOpType

D = 1024
SEQ = 8192
CTX = 256
NTOK = SEQ + CTX
OWN = 2048
NH = 8
DFF = 2816
EPS = 1e-6
NKT = NTOK // 128
SM_SCALE = 192.0 ** -0.5

C_NMIX, C_NFFN, C_BMOD, C_CONVW, C_CONVB, C_BGATE = 0, 8, 16, 48, 72, 80
C_QAN, C_KVAN = 96, 99
C_QN_NOPE, C_QN_ROPE, C_QN_ROPESW, C_KN_NOPE, C_KN_ROPE, C_KN_ROPESW = 101, 102, 103, 104, 105, 106
C_MASK, C_FIDX, C_SIGN, C_POS = 107, 109, 110, 111
NCONST = 240

ROPE_PERM = np.concatenate([np.arange(16, 32), np.arange(0, 16), np.arange(48, 64), np.arange(32, 48)])


class _Op:
    __slots__ = ("eng", "need", "is_dma", "key", "val")


class Reg:
    __slots__ = ("name", "last_w", "readers", "excl")

    def __init__(self, name, excl=False):
        self.name = name
        self.last_w = None
        self.readers = {}
        self.excl = excl


class Sched:
    ENGS = ("pe", "act", "dve", "pool", "sp")

    def __init__(self, nc, es):
        self.nc = nc
        self.es = es
        self.eng = {"pe": nc.tensor, "act": nc.scalar, "dve": nc.vector, "pool": nc.gpsimd, "sp": nc.sync}
        self.sem = {e: es.enter_context(nc.semaphore("sem_" + e)) for e in self.ENGS}
        self.cnt = {e: 0 for e in self.ENGS}
        self.waited = {e: {} for e in self.ENGS}
        self.dsem = {}
        self.dcnt = {}
        self.out_keys = set()

    def _wait(self, eng, semname, sem, val):
        w = self.waited[eng]
        if w.get(semname, 0) >= val:
            return
        w[semname] = val
        self.eng[eng].wait_ge(sem, val)

    def _dep(self, eng, d, raw):
        if d is None:
            return
        if d.is_dma:
            self._wait(eng, "d:" + d.key, self.dsem[d.key], d.val)
            return
        if d.eng == eng:
            if eng == "pe" or not raw:
                return
        assert self.cnt[d.eng] >= d.need, "dependency on an op not yet covered by a signal"
        self._wait(eng, "e:" + d.eng, self.sem[d.eng], d.need)

    def _deps(self, eng, reads, writes):
        for r in reads:
            self._dep(eng, r.last_w, True)
            if r.excl:
                for e2, rd in r.readers.items():
                    if e2 != eng and not isinstance(rd, list):
                        self._dep(eng, rd, False)
        for w in writes:
            self._dep(eng, w.last_w, False)
            for rd in w.readers.values():
                if isinstance(rd, list):
                    for x in rd:
                        self._dep(eng, x, False)
                else:
                    self._dep(eng, rd, False)

    def _record(self, op, reads, writes):
        for r in reads:
            if op.is_dma:
                r.readers.setdefault("dma", []).append(op)
            else:
                r.readers[op.eng] = op
        for w in writes:
            w.last_w = op
            w.readers = {}

    def op(self, eng, fn, reads=(), writes=(), sig=True):
        self._deps(eng, reads, writes)
        ins = fn()
        o = _Op()
        o.eng = eng
        o.is_dma = False
        o.key = None
        o.val = 0
        if sig:
            self.cnt[eng] += 1
            ins.then_inc(self.sem[eng], 1)
            o.need = self.cnt[eng]
        else:
            o.need = self.cnt[eng] + 1
        self._record(o, reads, writes)
        return o

    def dma(self, queue, key, out, in_, reads=(), writes=(), is_out=False):
        if key not in self.dsem:
            self.dsem[key] = self.es.enter_context(self.nc.semaphore("dsem_" + key))
            self.dcnt[key] = 0
        self._deps(queue, reads, writes)
        ins = self.eng[queue].dma_start(out=out, in_=in_)
        self.dcnt[key] += 16
        ins.then_inc(self.dsem[key], 16)
        o = _Op()
        o.eng = queue
        o.is_dma = True
        o.key = key
        o.val = self.dcnt[key]
        o.need = 0
        self._record(o, reads, writes)
        if is_out:
            self.out_keys.add(key)
        return o

    def barrier(self, skip_keys=()):
        for e in self.ENGS:
            for e2 in self.ENGS:
                if e2 != e and self.cnt[e2] > 0:
                    self._wait(e, "e:" + e2, self.sem[e2], self.cnt[e2])
            for k, s in self.dsem.items():
                if self.dcnt[k] > 0 and k not in skip_keys:
                    self._wait(e, "d:" + k, s, self.dcnt[k])

    def finish(self):
        for k in sorted(self.out_keys):
            self._wait("sp", "d:" + k, self.dsem[k], self.dcnt[k])


class _Stop(Exception):
    pass


def build_nc(stop=None, dumps=()):
    nc = bass.Bass("TRN2", target_bir_lowering=False)
    try:
        _build(nc, stop, dumps)
    except _Stop:
        pass
    return nc


def _build(nc, stop, dumps):

    def din(name, shape):
        return nc.dram_tensor(name, list(shape), F32, kind="ExternalInput").ap()

    d_xall = din("xall", (NTOK, D))
    d_consts = din("consts", (128, NCONST))
    d_cT = din("cT", (128, 16))
    d_g12b = din("g12b", (128, 2 * D))
    d_wmod = din("w_mod", (D, 6 * D))
    d_win = din("w_in", (D, 5824))
    d_wkr = din("w_kr", (D, 256))
    d_wqb = din("w_qb", (384, NH * 384))
    d_wkvb = din("w_kvb", (256, NH * 256))
    d_wco = din("w_co", (D, D))
    d_wao = din("w_ao", (D, D))
    d_wout = din("w_out", (D, D))
    d_wfi = din("w_fi", (D, 2 * DFF))
    d_wfo = din("w_fo", (DFF, D))
    d_out = nc.dram_tensor("out", [OWN, D], F32, kind="ExternalOutput").ap()
    d_xmid = nc.dram_tensor("xmid", [OWN, D], F32, kind="Internal").ap()

    with ExitStack() as es:
        E = es.enter_context
        S = Sched(nc, es)
        pe, act, dve, pool = nc.tensor, nc.scalar, nc.vector, nc.gpsimd
        NAMES = {}

        def checkpoint(name):
            if stop != name:
                return
            S.barrier()
            for dn in dumps:
                ap = NAMES[dn]
                shp = list(ap.shape)
                dt_ = ap.dtype
                d = nc.dram_tensor("dbg_" + dn, shp, dt_, kind="ExternalOutput").ap()
                S.dma("sp", "dbg", d, ap, is_out=True)
            S.finish()
            raise _Stop()

        def sb(name, cols, dt=F32, parts=128):
            return E(nc.sbuf_tensor(name, [parts, cols], dt))

        cst = sb("cst", NCONST)
        G12 = sb("G12", 2 * D)
        cTs = sb("cTs", 16)
        scf = sb("scf", 16)
        scb = sb("scb", 16, BF16)
        pm = sb("pm", 64)
        A1 = sb("A1", 16)
        A2 = sb("A2", 16)
        identb = sb("identb", 128, BF16)
        onesb = sb("onesb", 128, BF16)
        oneshi = sb("oneshi", 128, BF16)
        onesf = sb("onesf", 128)
        epsc = sb("epsc", 2)
        frq = sb("frq", 4)
        ang = sb("ang", 128)
        angk = sb("angk", 128)
        COS = sb("COS", 128)
        SIN = sb("SIN", 128)
        gq2 = sb("gq2", 4)
        ssc = sb("ssc", 8)
        rr = [sb("rr%d" % i, 512)[:, :] for i in range(4)]
        tmpA = sb("tmpA", 512)[:, :]
        tmpB = sb("tmpB", 512)[:, :]
        sqs = [sb("sq%d" % i, 3 * 512, BF16)[:, :] for i in range(2)]
        BA = sb("BA", 152 * 512, BF16)
        FA = sb("FA", 4104)

        def bav(off_kb, cols, parts=None):
            o = int(off_kb * 512)
            v = BA[:, o:o + cols]
            return v

        R = {}

        def reg(name):
            if name not in R:
                R[name] = Reg(name)
            return R[name]

        pbank = [E(nc.psum_tensor("pb%d" % i, [128, 512], F32)) for i in range(8)]
        preg = [Reg("pb%d" % i, excl=True) for i in range(8)]

        def tpv(i):
            return pbank[i][:, :].bitcast(BF16)

        r_cst, r_g12b, r_cT = reg("cst"), reg("G12"), reg("cTs")
        S.dma("sp", "init0", cst[:, :], d_consts[:, :], writes=[r_cst])
        S.dma("sp", "init1", cTs[:, :], d_cT[:, :], writes=[r_cT])
        S.dma("sp", "init2", G12[:, :], d_g12b[:, :], writes=[r_g12b])

        r_misc = reg("misc")
        S.op("pool", lambda: pool.memset(tmpA[:, 0:128], 0.0), writes=[reg("tmpA")])
        S.op("pool", lambda: pool.affine_select(out=tmpA[:, 0:128], in_=tmpA[:, 0:128], pattern=[[-1, 128]],
                                                compare_op=ALU.not_equal, fill=1.0, base=0, channel_multiplier=1),
             reads=[reg("tmpA")], writes=[reg("tmpA")])
        S.op("dve", lambda: dve.tensor_copy(out=identb[:, :], in_=tmpA[:, 0:128]), reads=[reg("tmpA")], writes=[reg("identb")])
        S.op("dve", lambda: dve.memset(onesb[:, :], 1.0), writes=[reg("onesb")])
        S.op("dve", lambda: dve.memset(oneshi[:, :], 1.0), writes=[reg("oneshi")])
        S.op("dve", lambda: dve.memset(onesf[:, :], 1.0), writes=[reg("onesf")])
        S.op("dve", lambda: dve.memset(epsc[:, :], EPS), writes=[reg("epsc")])
        S.op("dve", lambda: dve.memset(oneshi[0:64, :], 0.0), writes=[reg("oneshi")])

        r_tab = reg("tab")
        S.op("act", lambda: act.activation(out=frq[:, 0:1], in_=cst[:, C_FIDX:C_FIDX + 1], func=AF.Exp,
                                           scale=-math.log(10000.0) / 16.0), reads=[r_cst], writes=[reg("frq")])
        TWO_PI = 2.0 * math.pi
        MAGIC = 12582912.0

        def sin_table(dst, shift, rname):
            S.op("dve", lambda: dve.tensor_scalar(out=ang[:, :], in0=cst[:, C_POS:C_POS + 128], scalar1=frq[:, 0:1],
                                                  scalar2=shift, op0=ALU.mult, op1=ALU.add),
                 reads=[r_cst, reg("frq")], writes=[reg("ang")])
            S.op("dve", lambda: dve.tensor_scalar(out=angk[:, :], in0=ang[:, :], scalar1=1.0 / TWO_PI, scalar2=MAGIC,
                                                  op0=ALU.mult, op1=ALU.add), reads=[reg("ang")], writes=[reg("angk")])
            S.op("dve", lambda: dve.tensor_scalar(out=angk[:, :], in0=angk[:, :], scalar1=MAGIC, scalar2=-TWO_PI,
                                                  op0=ALU.subtract, op1=ALU.mult), reads=[reg("angk")], writes=[reg("angk")])
            S.op("dve", lambda: dve.tensor_tensor(out=ang[:, :], in0=ang[:, :], in1=angk[:, :], op=ALU.add),
                 reads=[reg("ang"), reg("angk")], writes=[reg("ang")])
            S.op("dve", lambda: dve.tensor_scalar(out=ang[:, :], in0=ang[:, :], scalar1=-3.14159, scalar2=3.14159,
                                                  op0=ALU.max, op1=ALU.min), reads=[reg("ang")], writes=[reg("ang")])
            S.op("act", lambda: act.activation(out=dst[:, :], in_=ang[:, :], func=AF.Sin), reads=[reg("ang")], writes=[reg(rname)])

        sin_table(COS, math.pi / 2.0, "COS")
        sin_table(SIN, 0.0, "SINraw")
        S.op("dve", lambda: dve.tensor_scalar(out=SIN[:, :], in0=SIN[:, :], scalar1=cst[:, C_SIGN:C_SIGN + 1], scalar2=None,
                                              op0=ALU.mult), reads=[reg("SINraw"), r_cst], writes=[reg("SIN")])
        S.op("dve", lambda: dve.tensor_tensor(out=gq2[:, 0:1], in0=cst[:, C_QN_NOPE:C_QN_NOPE + 1],
                                              in1=cst[:, C_KN_NOPE:C_KN_NOPE + 1], op=ALU.mult), reads=[r_cst], writes=[reg("gq2")])

        S.op("act", lambda: act.activation(out=scf[:, :], in_=cTs[:, :], func=AF.Silu), reads=[r_cT], writes=[reg("scf")])
        S.op("dve", lambda: dve.tensor_copy(out=scb[:, :], in_=scf[:, :]), reads=[reg("scf")], writes=[reg("scb")])
        screp = sb("screp", 8 * 128, BF16)[:, :]
        r_screp = reg("screp")
        S.op("dve", lambda: dve.tensor_copy(out=screp.rearrange("p (k m) -> p k m", m=128),
                                            in_=bass.AP(scf, 0, [[16, 128], [2, 8], [0, 128]])),
             reads=[reg("scf")], writes=[r_screp])

        wm = [bav(113.5, 8 * 1024), bav(129.5, 8 * 1024)]
        r_wm = [reg("wm0"), reg("wm1")]
        r_pm = reg("pm")
        r_A = reg("A12")

        def wm_load(gi, s):
            S.dma("pool", "wm%d" % s, wm[s].rearrange("p (k n) -> p k n", k=8),
                  d_wmod[:, gi * 1024:(gi + 1) * 1024].rearrange("(k p) n -> p k n", p=128), writes=[r_wm[s]])

        def wm_fm(s, pi):
            for m in range(8):
                for k in range(8):
                    last = (m == 7 and k == 7)
                    S.op("pe", lambda m=m, k=k: pe.matmul(
                        pbank[0][:, (pi * 8 + m) * 2:(pi * 8 + m) * 2 + 2],
                        lhsT=wm[s][:, k * 1024 + m * 128:k * 1024 + (m + 1) * 128],
                        rhs=scb[:, k * 2:k * 2 + 2], start=(k == 0), stop=(k == 7)),
                        reads=[r_wm[s], reg("scb")], writes=[preg[0]], sig=last)

        def wm_tm(s, gg, extra_reads=()):
            for half in range(2):
                bk = 1 + half
                for k in range(8):
                    S.op("pe", lambda k=k: pe.matmul(
                        pbank[bk][:, :], lhsT=screp[:, k * 128:(k + 1) * 128],
                        rhs=wm[s][:, k * 1024 + half * 512:k * 1024 + (half + 1) * 512], start=(k == 0), stop=(k == 7)),
                        reads=[r_wm[s], r_screp] + list(extra_reads), writes=[preg[bk]], sig=(k == 7))
                o0 = gg * 1024 + half * 512
                S.op("dve", lambda: dve.tensor_tensor(out=G12[:, o0:o0 + 512], in0=pbank[bk][:, :],
                                                      in1=G12[:, o0:o0 + 512], op=ALU.add),
                     reads=[preg[bk], r_g12b], writes=[r_g12b])

        def pm_finish(lo):
            S.op("dve", lambda: dve.tensor_tensor(out=pm[:, lo:lo + 32].rearrange("p (a n) -> p a n", n=2),
                                                  in0=pbank[0][:, lo:lo + 32].rearrange("p (a n) -> p a n", n=2),
                                                  in1=bass.AP(cst, C_BMOD + lo // 2, [[NCONST, 128], [1, 16], [0, 2]]), op=ALU.add),
                 reads=[preg[0], r_cst], writes=[r_pm])

        def a_finish(Adst, col0, cn):
            S.op("dve", lambda: dve.tensor_scalar(out=Adst[:, :], in0=pm[:, col0:col0 + 16], scalar1=1.0, scalar2=None, op0=ALU.add),
                 reads=[r_pm], writes=[r_A])
            S.op("dve", lambda: dve.tensor_tensor(out=Adst[:, :].rearrange("p (a n) -> p a n", n=2),
                                                  in0=Adst[:, :].rearrange("p (a n) -> p a n", n=2),
                                                  in1=bass.AP(cst, cn, [[NCONST, 128], [1, 8], [0, 2]]), op=ALU.mult),
                 reads=[r_A, r_cst], writes=[r_A])

        wm_load(0, 0)
        wm_load(1, 1)
        wm_fm(0, 0)
        wm_fm(1, 1)
        pm_finish(0)
        a_finish(A1, 16, C_NMIX)
        NAMES.update(pm=pm[:, :], A1=A1[:, :], A2=A2[:, :], G12=G12[:, :], COS=COS[:, :], SIN=SIN[:, :], identb=identb[:, :])
        checkpoint("p0")

        xts = [FA[:, i * 1024:(i + 1) * 1024] for i in range(3)]
        xts += [bav(94 + 4 * i, 2048).bitcast(F32) for i in range(3)]
        r_xt = [reg("xt%d" % i) for i in range(6)]
        junk = bav(146, 1024)
        xnb = [bav(148, 1024), bav(150, 1024)]
        r_xn = [reg("xn0"), reg("xn1")]
        TPB = (6, 7)
        state = {"xt": 0, "xn": 0, "tp": 0, "ss": 0, "pb": 0}

        def norm_A(src_reg, src_ap):
            si = state["ss"] % 2
            state["ss"] += 1
            xi = state["xn"] % 2
            state["xn"] += 1
            r_ss = reg("ss%d" % si)
            c0 = si * 4
            S.op("act", lambda: act.activation(out=junk, in_=src_ap, func=AF.Square, accum_out=ssc[:, c0:c0 + 1]),
                 reads=[src_reg], writes=[reg("junk"), r_ss])
            S.op("act", lambda: act.activation(out=ssc[:, c0 + 1:c0 + 2], in_=ssc[:, c0:c0 + 1], func=AF.Ln,
                                               bias=epsc[:, 0:1], scale=1.0 / D), reads=[r_ss, reg("epsc")], writes=[r_ss])
            S.op("act", lambda: act.activation(out=ssc[:, c0 + 2:c0 + 3], in_=ssc[:, c0 + 1:c0 + 2], func=AF.Exp, scale=-0.5),
                 reads=[r_ss], writes=[r_ss])
            S.op("dve", lambda: dve.tensor_scalar(out=xnb[xi], in0=src_ap, scalar1=ssc[:, c0 + 2:c0 + 3], scalar2=None,
                                                  op0=ALU.mult), reads=[src_reg, r_ss], writes=[r_xn[xi]])
            return xi

        def norm_B(xi, Acol, Scol, dst_fn, dst_reg, ncols=128, force_act=False):
            ti = TPB[state["tp"] % 2]
            use_act = force_act or (state["tp"] % 3 == 0)
            state["tp"] += 1
            tv = tpv(ti)
            for j in range(8):
                S.op("pe", lambda j=j: pe.transpose(out=tv[:, j * 128:(j + 1) * 128], in_=xnb[xi][:, j * 128:(j + 1) * 128],
                                                    identity=identb[:, :]),
                     reads=[r_xn[xi], reg("identb")], writes=[preg[ti]], sig=(j == 7))
            for j in range(8):
                if use_act:
                    S.op("act", lambda j=j: act.activation(out=dst_fn(j), in_=tv[:, j * 128:j * 128 + ncols], func=AF.Identity,
                                                           bias=Scol(j), scale=Acol(j)),
                         reads=[preg[ti], r_A, r_pm], writes=[dst_reg])
                else:
                    S.op("dve", lambda j=j: dve.tensor_scalar(out=dst_fn(j), in0=tv[:, j * 128:j * 128 + ncols], scalar1=Acol(j),
                                                              scalar2=Scol(j), op0=ALU.mult, op1=ALU.add),
                         reads=[preg[ti], r_A, r_pm], writes=[dst_reg])

        def norm_transpose(src_reg, src_ap, Acol, Scol, dst_fn, dst_reg, ncols=128):
            xi = norm_A(src_reg, src_ap)
            norm_B(xi, Acol, Scol, dst_fn, dst_reg, ncols)

        def load_x(row0, nrows=128):
            i = state["xt"] % 6
            state["xt"] += 1
            S.dma("sp", "xt%d" % i, xts[i][0:nrows, :], d_xall[row0:row0 + nrows, :], writes=[r_xt[i]])
            return i

        def nextbank():
            b = state["pb"] % 6
            state["pb"] += 1
            return b

        kvnT = bav(0, 2 * NTOK)
        pack = bav(33, NTOK)
        qnT = bav(49.5, 3 * OWN)
        hxT = [bav(62, 8 * 512), bav(70, 8 * 512)]
        wA = bav(78, 8 * 704)
        wKR = bav(89, 8 * 256)
        CGk = FA[:, 3072:3584]
        SGk = FA[:, 3584:4096]
        r_hx = [reg("hxT0"), reg("hxT1")]
        r_wA, r_wKR = reg("wA"), reg("wKR")
        S.dma("pool", "wA", wA.rearrange("p (k n) -> p k n", k=8),
              d_win[:, 3072:3776].rearrange("(k p) n -> p k n", p=128), writes=[r_wA])
        S.dma("pool", "wKR", wKR.rearrange("p (k n) -> p k n", k=8),
              d_wkr[:, :].rearrange("(k p) n -> p k n", p=128), writes=[r_wKR])
        r_tk = reg("tabk")
        S.op("dve", lambda: dve.memset(FA[:, 3072:4096], 0.0), writes=[r_tk])
        S.op("dve", lambda: dve.tensor_scalar(out=CGk[32:64, :].rearrange("p (r c) -> p r c", c=64),
                                              in0=bass.AP(COS, 32 * 128, [[128, 32], [0, 8], [1, 64]]),
                                              scalar1=cst[32:64, C_KN_ROPE:C_KN_ROPE + 1], scalar2=None, op0=ALU.mult),
             reads=[reg("COS"), r_cst], writes=[r_tk])
        S.op("dve", lambda: dve.tensor_scalar(out=SGk[32:64, :].rearrange("p (r c) -> p r c", c=64),
                                              in0=bass.AP(SIN, 32 * 128, [[128, 32], [0, 8], [1, 64]]),
                                              scalar1=cst[32:64, C_KN_ROPESW:C_KN_ROPESW + 1], scalar2=None, op0=ALU.mult),
             reads=[reg("SIN"), r_cst], writes=[r_tk])

        def row_tables(Cg, Sg, g, c_cos, c_sin, rg):
            S.op("dve", lambda: dve.tensor_scalar(out=Cg[0:32, :].rearrange("p (r c) -> p r c", c=64),
                                                  in0=bass.AP(COS, g * 8, [[128, 32], [1, 8], [0, 64]]),
                                                  scalar1=cst[0:32, c_cos:c_cos + 1], scalar2=None, op0=ALU.mult),
                 reads=[reg("COS"), r_cst], writes=[rg])
            S.op("dve", lambda: dve.tensor_scalar(out=Sg[0:32, :].rearrange("p (r c) -> p r c", c=64),
                                                  in0=bass.AP(SIN, g * 8, [[128, 32], [1, 8], [0, 64]]),
                                                  scalar1=cst[0:32, c_sin:c_sin + 1], scalar2=None, op0=ALU.mult),
                 reads=[reg("SIN"), r_cst], writes=[rg])

        def rstd_bcast(stat_bank, scale, n):
            ri = state.setdefault("rr", 0) % 2
            state["rr"] = ri + 1
            a, b = rr[ri * 2], rr[ri * 2 + 1]
            ra, rb = reg("rr%d" % (ri * 2)), reg("rr%d" % (ri * 2 + 1))
            S.op("act", lambda: act.activation(out=a[:, 0:n], in_=pbank[stat_bank][:, 0:n], func=AF.Ln, bias=epsc[:, 0:1], scale=scale),
                 reads=[preg[stat_bank], reg("epsc")], writes=[ra])
            S.op("act", lambda: act.activation(out=b[:, 0:n], in_=a[:, 0:n], func=AF.Exp, scale=-0.5), reads=[ra], writes=[rb])
            return b, rb

        groups = [(g, g * 512, 512, False) for g in range(16)] + [(16, SEQ, 256, True)]
        slot_of = {}
        issued = [0]

        def ensure(upto):
            while issued[0] <= min(upto, NKT - 1):
                slot_of[issued[0]] = load_x(issued[0] * 128)
                issued[0] += 1

        def stage_A(T):
            ensure(T + 5)
            xi = slot_of[T]
            return norm_A(r_xt[xi], xts[xi])

        def post(gi):
            g, col0, n, is_ctx = groups[gi]
            own = g < 4
            hs = gi % 2
            def proj(bk, wt, rw, wcols, c_off):
                for k in range(8):
                    S.op("pe", lambda k=k: pe.matmul(pbank[bk][:, 0:n], lhsT=wt[:, k * wcols + c_off:k * wcols + c_off + 128],
                                                     rhs=hxT[hs][:, k * 512:k * 512 + n], start=(k == 0), stop=(k == 7)),
                         reads=[rw, r_hx[hs]], writes=[preg[bk]], sig=(k == 7))

            bkv = [nextbank(), nextbank()]
            for c in range(2):
                proj(bkv[c], wA, r_wA, 704, 384 + c * 128)
                yield
            bkr, bks = nextbank(), nextbank()
            proj(bkr, wKR, r_wKR, 256, 0)
            yield
            proj(bks, wKR, r_wKR, 256, 128)
            yield
            sq = sqs[gi % 2]
            r_sq = reg("sq%d" % (gi % 2))
            for c in range(2):
                S.op("act", lambda c=c: act.activation(out=sq[:, c * 512:c * 512 + n], in_=pbank[bkv[c]][:, 0:n], func=AF.Square),
                     reads=[preg[bkv[c]]], writes=[r_sq])
            bst = nextbank()
            for c in range(2):
                S.op("pe", lambda c=c: pe.matmul(pbank[bst][:, 0:n], lhsT=onesb[:, :], rhs=sq[:, c * 512:c * 512 + n],
                                                 start=(c == 0), stop=(c == 1)),
                     reads=[r_sq, reg("onesb")], writes=[preg[bst]], sig=(c == 1))
            rb, rrb = rstd_bcast(bst, 1.0 / 256.0, n)
            r_kvn = reg("kvn%d" % g)
            for c in range(2):
                S.op("dve", lambda c=c: dve.scalar_tensor_tensor(out=kvnT[:, c * NTOK + col0:c * NTOK + col0 + n],
                                                                 in0=pbank[bkv[c]][:, 0:n], scalar=cst[:, C_KVAN + c:C_KVAN + c + 1],
                                                                 in1=rb[:, 0:n], op0=ALU.mult, op1=ALU.mult),
                     reads=[preg[bkv[c]], rrb, r_cst], writes=[r_kvn])
            r_pk = reg("pack%d" % g)
            S.op("act", lambda: act.activation(out=pack[64:128, col0:col0 + n], in_=pbank[bkr][64:128, 0:n], func=AF.Square),
                 reads=[preg[bkr]], writes=[r_pk])
            if is_ctx:
                S.op("dve", lambda: dve.tensor_scalar(out=pack[0:64, col0:col0 + n], in0=pbank[bkr][0:64, 0:n],
                                                      scalar1=cst[0:64, C_KN_ROPE:C_KN_ROPE + 1], scalar2=None, op0=ALU.mult),
                     reads=[preg[bkr], r_cst], writes=[r_pk])
            else:
                row_tables(CGk, SGk, g, C_KN_ROPE, C_KN_ROPESW, r_tk)
                S.op("dve", lambda: dve.tensor_tensor(out=tmpA[0:64, 0:n], in0=pbank[bkr][0:64, 0:n], in1=CGk[0:64, 0:n], op=ALU.mult),
                     reads=[preg[bkr], r_tk], writes=[reg("tmpA")])
                S.op("dve", lambda: dve.tensor_tensor(out=tmpB[0:64, 0:n], in0=pbank[bks][0:64, 0:n], in1=SGk[0:64, 0:n], op=ALU.mult),
                     reads=[preg[bks], r_tk], writes=[reg("tmpB")])
                S.op("dve", lambda: dve.tensor_tensor(out=pack[0:64, col0:col0 + n], in0=tmpA[0:64, 0:n], in1=tmpB[0:64, 0:n], op=ALU.add),
                     reads=[reg("tmpA"), reg("tmpB")], writes=[r_pk])
            if own:
                bq = [nextbank(), nextbank(), nextbank()]
                for c in range(3):
                    proj(bq[c], wA, r_wA, 704, c * 128)
                    yield
                for c in range(3):
                    S.op("act", lambda c=c: act.activation(out=sq[:, c * 512:c * 512 + n], in_=pbank[bq[c]][:, 0:n], func=AF.Square),
                         reads=[preg[bq[c]]], writes=[r_sq])
                bst = nextbank()
                for c in range(3):
                    S.op("pe", lambda c=c: pe.matmul(pbank[bst][:, 0:n], lhsT=onesb[:, :], rhs=sq[:, c * 512:c * 512 + n],
                                                     start=(c == 0), stop=(c == 2)),
                         reads=[r_sq, reg("onesb")], writes=[preg[bst]], sig=(c == 2))
                rb, rrb = rstd_bcast(bst, 1.0 / 384.0, n)
                r_qn = reg("qn%d" % g)
                for c in range(3):
                    S.op("dve", lambda c=c: dve.scalar_tensor_tensor(out=qnT[:, c * OWN + col0:c * OWN + col0 + n],
                                                                     in0=pbank[bq[c]][:, 0:n], scalar=cst[:, C_QAN + c:C_QAN + c + 1],
                                                                     in1=rb[:, 0:n], op0=ALU.mult, op1=ALU.mult),
                         reads=[preg[bq[c]], rrb, r_cst], writes=[r_qn])

        hA = stage_A(0)
        pend = None
        for gi, (g, col0, n, is_ctx) in enumerate(groups):
            hs = gi % 2
            nn = 1 if is_ctx else 0
            for ti in range(n // 128):
                T = col0 // 128 + ti
                hN = stage_A(T + 1) if T + 1 < NKT else None
                norm_B(hA, lambda j: A1[:, j * 2 + nn:j * 2 + nn + 1], lambda j: pm[:, j * 2 + nn:j * 2 + nn + 1],
                       lambda j, ti=ti: hxT[hs][:, j * 512 + ti * 128:j * 512 + (ti + 1) * 128], r_hx[hs])
                hA = hN
                if pend is not None:
                    next(pend, None)
            if pend is not None:
                for _ in pend:
                    pass
            pend = post(gi)
            if gi == 6:
                wm_load(3, 0)
                wm_load(4, 1)
        for _ in pend:
            pass
        wm_fm(0, 2)
        wm_fm(1, 3)
        pm_finish(32)
        a_finish(A2, 48, C_NFFN)
        wm_load(2, 0)
        wm_load(5, 1)
        S.barrier(skip_keys=("wm0", "wm1"))
        NAMES.update(kvnT=kvnT, pack=pack, qnT=qnT)
        checkpoint("p1")

        KTn = bav(62, NTOK)
        Vh = bav(78.5, NTOK)
        QTn = bav(95, OWN)
        QTr = bav(99, OWN)
        PT = [bav(103 + i, 512) for i in range(4)]
        wkv = [bav(107, 2 * 256), bav(108, 2 * 256)]
        wq = [bav(109, 3 * 384), bav(111.25, 3 * 384)]
        attnT = bav(113.5, NH * OWN)
        CGq = FA[:, 0:512]
        SGq = FA[:, 512:1024]
        rl = FA[:, 1024:1536]
        sclk = FA[:, 1536:1536 + NKT]
        sclt = FA[:, 1664:1664 + NKT]
        r_KTn, r_Vh, r_QTn, r_QTr, r_attn = reg("KTn"), reg("Vh"), reg("QTn"), reg("QTr"), reg("attnT")
        r_PT = [reg("PT%d" % i) for i in range(4)]
        r_wkv = [reg("wkv0"), reg("wkv1")]
        r_wq = [reg("wq0"), reg("wq1")]
        r_tq = reg("tabq")
        r_scl = reg("sclk")
        S.op("dve", lambda: dve.memset(QTr, 0.0), writes=[r_QTr])
        S.op("dve", lambda: dve.memset(FA[:, 0:1024], 0.0), writes=[r_tq])
        S.op("dve", lambda: dve.tensor_scalar(out=CGq[32:64, :].rearrange("p (r c) -> p r c", c=64),
                                              in0=bass.AP(COS, 32 * 128, [[128, 32], [0, 8], [1, 64]]),
                                              scalar1=cst[32:64, C_QN_ROPE:C_QN_ROPE + 1], scalar2=None, op0=ALU.mult),
             reads=[reg("COS"), r_cst], writes=[r_tq])
        S.op("dve", lambda: dve.tensor_scalar(out=SGq[32:64, :].rearrange("p (r c) -> p r c", c=64),
                                              in0=bass.AP(SIN, 32 * 128, [[128, 32], [0, 8], [1, 64]]),
                                              scalar1=cst[32:64, C_QN_ROPESW:C_QN_ROPESW + 1], scalar2=None, op0=ALU.mult),
             reads=[reg("SIN"), r_cst], writes=[r_tq])

        def load_head_w(h):
            s = h % 2
            S.dma("pool", "wkv%d" % s, wkv[s].rearrange("p (c n) -> p c n", c=2),
                  d_wkvb[:, h * 256:(h + 1) * 256].rearrange("(c p) n -> p c n", p=128), writes=[r_wkv[s]])
            S.dma("pool", "wq%d" % s, wq[s].rearrange("p (c n) -> p c n", c=3),
                  d_wqb[:, h * 384:(h + 1) * 384].rearrange("(c p) n -> p c n", p=128), writes=[r_wq[s]])

        SB_, OB_, LB_, XB_ = (0, 1, 2, 5), (3, 4), 6, 7
        NPT = len(PT)
        accs = [FA[:, 2048:2560], FA[:, 2560:3072], FA[:, 3072:3584], FA[:, 3584:4096]]
        r_acc = [reg("acc%d" % i) for i in range(4)]
        load_head_w(0)
        it = 0
        for h in range(NH):
            s = h % 2
            if h + 1 < NH:
                load_head_w(h + 1)
            QB = ((0, 1, 2), (5, 3, 4))

            def q_mm(g):
                c0 = g * 512
                for (bk, off) in zip(QB[g % 2], (0, 128, 256)):
                    for c in range(3):
                        S.op("pe", lambda c=c, bk=bk, off=off: pe.matmul(
                            pbank[bk][:, :], lhsT=wq[s][:, c * 384 + off:c * 384 + off + 128],
                            rhs=qnT[:, c * OWN + c0:c * OWN + c0 + 512], start=(c == 0), stop=(c == 2)),
                            reads=[r_wq[s], reg("qn%d" % g)], writes=[preg[bk]], sig=(c == 2))

            def q_post(g):
                c0 = g * 512
                bn, br, bs = QB[g % 2]
                sq = sqs[g % 2]
                r_sq = reg("sq%d" % (g % 2))
                S.op("act", lambda: act.activation(out=sq[:, 0:512], in_=pbank[bn][:, :], func=AF.Square), reads=[preg[bn]], writes=[r_sq])
                S.op("act", lambda: act.activation(out=sq[:, 512:1024], in_=pbank[br][:, :], func=AF.Square), reads=[preg[br]], writes=[r_sq])
                for c in range(2):
                    S.op("pe", lambda c=c: pe.matmul(pbank[XB_][:, :], lhsT=onesb[:, :], rhs=sq[:, c * 512:(c + 1) * 512],
                                                     start=(c == 0), stop=(c == 1)),
                         reads=[r_sq, reg("onesb")], writes=[preg[XB_]], sig=(c == 1))
                rb, rrb = rstd_bcast(XB_, 1.0 / 192.0, 512)
                S.op("dve", lambda: dve.scalar_tensor_tensor(out=QTn[:, c0:c0 + 512], in0=pbank[bn][:, :], scalar=gq2[:, 0:1],
                                                             in1=rb[:, :], op0=ALU.mult, op1=ALU.mult),
                     reads=[preg[bn], rrb, reg("gq2")], writes=[r_QTn])
                row_tables(CGq, SGq, g, C_QN_ROPE, C_QN_ROPESW, r_tq)
                S.op("dve", lambda: dve.tensor_tensor(out=tmpA[0:64, :], in0=pbank[br][0:64, :], in1=CGq[0:64, :], op=ALU.mult),
                     reads=[preg[br], r_tq], writes=[reg("tmpA")])
                S.op("dve", lambda: dve.tensor_tensor(out=tmpB[0:64, :], in0=pbank[bs][0:64, :], in1=SGq[0:64, :], op=ALU.mult),
                     reads=[preg[bs], r_tq], writes=[reg("tmpB")])
                S.op("dve", lambda: dve.tensor_tensor(out=tmpA[0:64, :], in0=tmpA[0:64, :], in1=tmpB[0:64, :], op=ALU.add),
                     reads=[reg("tmpA"), reg("tmpB")], writes=[reg("tmpA")])
                S.op("dve", lambda: dve.tensor_tensor(out=QTr[0:64, c0:c0 + 512], in0=tmpA[0:64, :], in1=rb[0:64, :], op=ALU.mult),
                     reads=[reg("tmpA"), rrb], writes=[r_QTr])

            q_mm(0)
            for g in range(4):
                if g + 1 < 4:
                    q_mm(g + 1)
                q_post(g)
            NAMES.update(QTn=QTn, QTr=QTr, KTn=KTn, Vh=Vh, sclk=sclk, attnT=attnT)
            checkpoint("p2q")
            KB = (0, 1, 2)
            VB = (5, 3)

            def k_mm(gi):
                g, col0, n, is_ctx = groups[gi]
                bk = KB[gi % 3]
                for c in range(2):
                    S.op("pe", lambda c=c: pe.matmul(pbank[bk][:, 0:n], lhsT=wkv[s][:, c * 256:c * 256 + 128],
                                                     rhs=kvnT[:, c * NTOK + col0:c * NTOK + col0 + n], start=(c == 0), stop=(c == 1)),
                         reads=[r_wkv[s], reg("kvn%d" % g)], writes=[preg[bk]], sig=(c == 1))

            def k_post(gi):
                g, col0, n, is_ctx = groups[gi]
                bk = KB[gi % 3]
                sq = sqs[gi % 2]
                r_sq = reg("sq%d" % (gi % 2))
                S.op("act", lambda: act.activation(out=sq[:, 0:n], in_=pbank[bk][:, 0:n], func=AF.Square), reads=[preg[bk]], writes=[r_sq])
                S.op("act", lambda: act.activation(out=KTn[:, col0:col0 + n], in_=pbank[bk][:, 0:n], func=AF.Copy), reads=[preg[bk]], writes=[r_KTn])
                nt = n // 128
                bv = VB[gi % 2]
                for ti in range(nt):
                    kt = col0 // 128 + ti
                    for c in range(2):
                        S.op("pe", lambda ti=ti, kt=kt, c=c: pe.matmul(
                            pbank[bv][:, ti * 128:(ti + 1) * 128], lhsT=kvnT[:, c * NTOK + kt * 128:c * NTOK + (kt + 1) * 128],
                            rhs=wkv[s][:, c * 256 + 128:c * 256 + 256], start=(c == 0), stop=(c == 1)),
                            reads=[r_wkv[s], reg("kvn%d" % g)], writes=[preg[bv]], sig=(c == 1))
                for ti in range(nt):
                    kt = col0 // 128 + ti
                    S.op("pe", lambda ti=ti, kt=kt: pe.matmul(pbank[XB_][:, 2 * kt:2 * kt + 2], lhsT=sq[:, ti * 128:(ti + 1) * 128],
                                                              rhs=onesb[:, 0:2], start=True, stop=False),
                         reads=[r_sq, reg("onesb")], writes=[preg[XB_]], sig=False)
                    S.op("pe", lambda ti=ti, kt=kt: pe.matmul(pbank[XB_][:, 2 * kt:2 * kt + 2], lhsT=pack[:, kt * 128:(kt + 1) * 128],
                                                              rhs=oneshi[:, 0:2], start=False, stop=True),
                         reads=[reg("pack%d" % g), reg("oneshi")], writes=[preg[XB_]], sig=True)
                S.op("dve", lambda: dve.tensor_copy(out=Vh[:, col0:col0 + n], in_=pbank[bv][:, 0:n]), reads=[preg[bv]], writes=[r_Vh])

            k_mm(0)
            for gi in range(len(groups)):
                if gi + 1 < len(groups):
                    k_mm(gi + 1)
                k_post(gi)
            S.op("act", lambda: act.activation(out=sclt, in_=pbank[XB_][:, 0:2 * NKT:2], func=AF.Ln, bias=epsc[:, 0:1], scale=1.0 / 192.0),
                 reads=[preg[XB_], reg("epsc")], writes=[reg("sclt")])
            S.op("act", lambda: act.activation(out=sclk, in_=sclt, func=AF.Exp, scale=-0.5), reads=[reg("sclt")], writes=[r_scl])
            S.op("dve", lambda: dve.tensor_scalar(out=sclk, in0=sclk, scalar1=SM_SCALE, scalar2=None, op0=ALU.mult),
                 reads=[r_scl], writes=[r_scl])
            checkpoint("p2kv")
            if h == 0:
                wm_tm(0, 0, extra_reads=[r_attn])
                wm_tm(1, 1, extra_reads=[r_attn])
            for qt in range(4):
                q0 = qt * 512
                ob = OB_[it % 2]
                a0, a1 = (accs[0], accs[1]) if it % 2 == 0 else (accs[2], accs[3])
                ra0, ra1 = (r_acc[0], r_acc[1]) if it % 2 == 0 else (r_acc[2], r_acc[3])
                it += 1
                nd = 0

                def s_mm(i):
                    bk = SB_[i % 4]
                    S.op("pe", lambda: pe.matmul(pbank[bk][:, :], lhsT=KTn[:, i * 128:(i + 1) * 128], rhs=QTn[:, q0:q0 + 512],
                                                 start=True, stop=False),
                         reads=[r_KTn, r_QTn], writes=[preg[bk]], sig=False)
                    S.op("pe", lambda: pe.matmul(pbank[bk][:, :], lhsT=pack[:, i * 128:(i + 1) * 128], rhs=QTr[:, q0:q0 + 512],
                                                 start=False, stop=True),
                         reads=[reg("pack%d" % min(i // 4, 16)), r_QTr], writes=[preg[bk]], sig=True)

                LOOK = 3
                for i in range(LOOK):
                    s_mm(i)
                for i in range(NKT):
                    if i + LOOK < NKT:
                        s_mm(i + LOOK)
                    bk = SB_[i % 4]
                    p = i % NPT
                    S.op("act", lambda: act.activation(out=PT[p], in_=pbank[bk][:, :], func=AF.Exp, scale=sclk[:, i:i + 1]),
                         reads=[preg[bk], r_scl], writes=[r_PT[p]])
                    S.op("pe", lambda: pe.matmul(pbank[ob][:, :], lhsT=Vh[:, i * 128:(i + 1) * 128], rhs=PT[p],
                                                 start=(i == 0), stop=(i == NKT - 1)),
                         reads=[r_Vh, r_PT[p]], writes=[preg[ob]], sig=True)
                    if i % 16 == 15:
                        S.op("pe", lambda: pe.matmul(pbank[LB_][:, :], lhsT=onesb[:, :], rhs=PT[p], start=(i == 15), stop=False),
                             reads=[reg("onesb"), r_PT[p]], writes=[preg[LB_]], sig=True)
                    else:
                        aa, ra = (a0, ra0) if nd % 2 == 0 else (a1, ra1)
                        if nd < 2:
                            S.op("dve", lambda: dve.tensor_copy(out=aa, in_=PT[p]), reads=[r_PT[p]], writes=[ra])
                        else:
                            S.op("dve", lambda: dve.tensor_tensor(out=aa, in0=aa, in1=PT[p], op=ALU.add), reads=[ra, r_PT[p]], writes=[ra])
                        nd += 1
                S.op("dve", lambda: dve.tensor_tensor(out=a0, in0=a0, in1=a1, op=ALU.add), reads=[ra0, ra1], writes=[ra0])
                S.op("pe", lambda: pe.matmul(pbank[LB_][:, :], lhsT=onesf[:, :], rhs=a0, start=False, stop=True),
                     reads=[reg("onesf"), ra0], writes=[preg[LB_]], sig=True)
                S.op("act", lambda: act.activation(out=rl, in_=pbank[LB_][:, :], func=AF.Ln), reads=[preg[LB_]], writes=[reg("rl")])
                S.op("act", lambda: act.activation(out=rl, in_=rl, func=AF.Exp, scale=-1.0), reads=[reg("rl")], writes=[reg("rl")])
                S.op("dve", lambda: dve.tensor_tensor(out=attnT[:, h * OWN + q0:h * OWN + q0 + 512], in0=pbank[ob][:, :], in1=rl, op=ALU.mult),
                     reads=[preg[ob], reg("rl")], writes=[r_attn])
                checkpoint("p2a")
        S.barrier()
        NAMES.update(attnT=attnT, KTn=KTn, Vh=Vh, QTn=QTn, QTr=QTr)
        checkpoint("p2")

        HW_ = OWN + 2
        TMPS = [(tmpA, reg("tmpA")), (tmpB, reg("tmpB")), (rr[0], reg("rr0")), (rr[1], reg("rr1"))]
        hxTo = bav(0, 8 * HW_)
        vT = bav(33, 8 * OWN)
        mT = bav(65, 8 * OWN)
        wst = [bav(97, 8 * 512), bav(105, 8 * 512)]
        r_hxo, r_vT, r_mT = reg("hxTo"), reg("vT"), reg("mT")
        r_wst = [reg("wst0"), reg("wst1")]
        U = FA[:, 0:HW_]
        Y = FA[:, 2050:2050 + OWN]
        r_U, r_Y = reg("U"), reg("Y")
        S.op("dve", lambda: dve.memset(xts[2], 0.0), writes=[r_xt[2]])
        S.dma("sp", "xt2", xts[2][0:1, :], d_xall[SEQ - 1:SEQ, :], writes=[r_xt[2]])
        S.dma("sp", "xt2", xts[2][1:2, :], d_xall[OWN:OWN + 1, :], writes=[r_xt[2]])
        norm_transpose(r_xt[2], xts[2], lambda j: A1[:, j * 2:j * 2 + 1], lambda j: pm[:, j * 2:j * 2 + 1],
                       lambda j: hxTo[:, j * HW_:(j + 1) * HW_:HW_ - 1], r_hxo, ncols=2)
        S.dma("sp", "xt0", xts[0], d_xall[0:128, :], writes=[r_xt[0]])
        S.dma("sp", "xt1", xts[1], d_xall[128:256, :], writes=[r_xt[1]])
        hA = norm_A(r_xt[0], xts[0])
        for t in range(16):
            if t + 2 < 16:
                sl = (t + 2) % 3
                S.dma("sp", "xt%d" % sl, xts[sl], d_xall[(t + 2) * 128:(t + 3) * 128, :], writes=[r_xt[sl]])
            hN = norm_A(r_xt[(t + 1) % 3], xts[(t + 1) % 3]) if t + 1 < 16 else None
            norm_B(hA, lambda j: A1[:, j * 2:j * 2 + 1], lambda j: pm[:, j * 2:j * 2 + 1],
                   lambda j, t=t: hxTo[:, j * HW_ + 1 + t * 128:j * HW_ + 1 + (t + 1) * 128], r_hxo)
            hA = hN
        S.barrier()
        NAMES.update(hxTo=hxTo)
        checkpoint("p3i")
        pieces = [(0, 512), (512, 512), (1024, 512), (1536, 512), (2048, 2)]
        for j in range(8):
            s = j % 2
            for ci, cb in enumerate((0, 1024, 2048)):
                S.dma("pool", "wst%d" % s, wst[s][:, :].rearrange("p (k n) -> p k n", k=8)[:, :, ci * 128:(ci + 1) * 128],
                      d_win[:, cb + j * 128:cb + (j + 1) * 128].rearrange("(k p) n -> p k n", p=128), writes=[r_wst[s]])
            for (p0, pn) in pieces:
                bc, bx_ = nextbank(), nextbank()
                for (bk, off) in ((bc, 128), (bx_, 256)):
                    for k in range(8):
                        S.op("pe", lambda k=k, bk=bk, off=off: pe.matmul(
                            pbank[bk][:, 0:pn], lhsT=wst[s][:, k * 512 + off:k * 512 + off + 128],
                            rhs=hxTo[:, k * HW_ + p0:k * HW_ + p0 + pn], start=(k == 0), stop=(k == 7)),
                            reads=[r_wst[s], r_hxo], writes=[preg[bk]], sig=(k == 7))
                tsel = state.setdefault("tsel", 0)
                state["tsel"] = tsel + 1
                tX, rX = TMPS[tsel % 4]
                S.op("act", lambda: act.activation(out=tX[:, 0:pn], in_=pbank[bx_][:, 0:pn], func=AF.Copy),
                     reads=[preg[bx_]], writes=[rX])
                S.op("dve", lambda: dve.tensor_tensor(out=U[:, p0:p0 + pn], in0=pbank[bc][:, 0:pn], in1=tX[:, 0:pn], op=ALU.mult),
                     reads=[preg[bc], rX], writes=[r_U])
            S.op("dve", lambda: dve.tensor_scalar(out=U[:, 0:1], in0=U[:, 0:1], scalar1=cst[:, C_MASK:C_MASK + 1], scalar2=None,
                                                  op0=ALU.mult), reads=[r_U, r_cst], writes=[r_U])
            S.op("dve", lambda: dve.tensor_scalar(out=U[:, HW_ - 1:HW_], in0=U[:, HW_ - 1:HW_], scalar1=cst[:, C_MASK + 1:C_MASK + 2],
                                                  scalar2=None, op0=ALU.mult), reads=[r_U, r_cst], writes=[r_U])
            S.op("dve", lambda: dve.tensor_scalar(out=Y, in0=U[:, 0:OWN], scalar1=cst[:, C_CONVW + j:C_CONVW + j + 1],
                                                  scalar2=cst[:, C_CONVB + j:C_CONVB + j + 1], op0=ALU.mult, op1=ALU.add),
                 reads=[r_U, r_cst], writes=[r_Y])
            for kk in (1, 2):
                S.op("dve", lambda kk=kk: dve.scalar_tensor_tensor(out=Y, in0=U[:, kk:kk + OWN],
                                                                   scalar=cst[:, C_CONVW + kk * 8 + j:C_CONVW + kk * 8 + j + 1],
                                                                   in1=Y, op0=ALU.mult, op1=ALU.add),
                     reads=[r_U, r_Y, r_cst], writes=[r_Y])
            for q in range(4):
                bb = nextbank()
                for k in range(8):
                    S.op("pe", lambda k=k: pe.matmul(pbank[bb][:, :], lhsT=wst[s][:, k * 512:k * 512 + 128],
                                                     rhs=hxTo[:, k * HW_ + 1 + q * 512:k * HW_ + 1 + (q + 1) * 512],
                                                     start=(k == 0), stop=(k == 7)),
                         reads=[r_wst[s], r_hxo], writes=[preg[bb]], sig=(k == 7))
                S.op("dve", lambda: dve.tensor_tensor(out=vT[:, j * OWN + q * 512:j * OWN + (q + 1) * 512], in0=pbank[bb][:, :],
                                                      in1=Y[:, q * 512:(q + 1) * 512], op=ALU.mult),
                     reads=[preg[bb], r_Y], writes=[r_vT])
        for j in range(8):
            s = j % 2
            srcs = (d_win[:, 3776 + j * 128:3776 + (j + 1) * 128], d_win[:, 4800 + j * 128:4800 + (j + 1) * 128],
                    d_wco[:, j * 128:(j + 1) * 128], d_wao[:, j * 128:(j + 1) * 128])
            for ci, src in enumerate(srcs):
                S.dma("pool", "wst%d" % s, wst[s][:, :].rearrange("p (k n) -> p k n", k=8)[:, :, ci * 128:(ci + 1) * 128],
                      src.rearrange("(k p) n -> p k n", p=128), writes=[r_wst[s]])
            for q in range(4):
                c0 = q * 512
                bks = [nextbank() for _ in range(4)]
                rhs_of = (lambda k: hxTo[:, k * HW_ + 1 + c0:k * HW_ + 1 + c0 + 512],
                          lambda k: hxTo[:, k * HW_ + 1 + c0:k * HW_ + 1 + c0 + 512],
                          lambda k: vT[:, k * OWN + c0:k * OWN + c0 + 512],
                          lambda k: attnT[:, k * OWN + c0:k * OWN + c0 + 512])
                rregs = (r_hxo, r_hxo, r_vT, r_attn)
                for ci in range(4):
                    for k in range(8):
                        S.op("pe", lambda k=k, ci=ci: pe.matmul(pbank[bks[ci]][:, :],
                                                                lhsT=wst[s][:, k * 512 + ci * 128:k * 512 + (ci + 1) * 128],
                                                                rhs=rhs_of[ci](k), start=(k == 0), stop=(k == 7)),
                             reads=[r_wst[s], rregs[ci]], writes=[preg[bks[ci]]], sig=(k == 7))
                tsel = state.setdefault("tsel", 0)
                state["tsel"] = tsel + 1
                (tP, rP), (tQ, rQ) = TMPS[(2 * tsel) % 4], TMPS[(2 * tsel + 1) % 4]
                S.op("act", lambda: act.activation(out=tP, in_=pbank[bks[0]][:, :], func=AF.Sigmoid,
                                                   bias=cst[:, C_BGATE + j:C_BGATE + j + 1]),
                     reads=[preg[bks[0]], r_cst], writes=[rP])
                S.op("act", lambda: act.activation(out=tQ, in_=pbank[bks[1]][:, :], func=AF.Sigmoid,
                                                   bias=cst[:, C_BGATE + 8 + j:C_BGATE + 8 + j + 1]),
                     reads=[preg[bks[1]], r_cst], writes=[rQ])
                S.op("dve", lambda: dve.tensor_tensor(out=tP, in0=pbank[bks[2]][:, :], in1=tP, op=ALU.mult),
                     reads=[preg[bks[2]], rP], writes=[rP])
                S.op("dve", lambda: dve.tensor_tensor(out=tQ, in0=pbank[bks[3]][:, :], in1=tQ, op=ALU.mult),
                     reads=[preg[bks[3]], rQ], writes=[rQ])
                S.op("dve", lambda: dve.tensor_tensor(out=mT[:, j * OWN + c0:j * OWN + c0 + 512], in0=tP, in1=tQ, op=ALU.add),
                     reads=[rP, rQ], writes=[r_mT])
        S.barrier()
        NAMES.update(vT=vT, mT=mT)
        checkpoint("p3iii")
        wout = bav(0, 8 * D)
        hx2T = bav(16, 8 * OWN)
        r_wout, r_hx2 = reg("wout"), reg("hx2T")
        S.dma("pool", "wout", wout.rearrange("p (k n) -> p k n", k=8), d_wout[:, :].rearrange("(k p) n -> p k n", p=128), writes=[r_wout])
        xm = [FA[:, 2048:3072], FA[:, 3072:4096]]
        r_xm = [reg("xm0"), reg("xm1")]
        S.dma("sp", "xt0", xts[0], d_xall[0:128, :], writes=[r_xt[0]])

        def wout_mm(t):
            bo = [nextbank(), nextbank()]
            for half in range(2):
                for k in range(8):
                    S.op("pe", lambda k=k, half=half: pe.matmul(pbank[bo[half]][:, :], lhsT=mT[:, k * OWN + t * 128:k * OWN + (t + 1) * 128],
                                                                rhs=wout[:, k * D + half * 512:k * D + (half + 1) * 512],
                                                                start=(k == 0), stop=(k == 7)),
                         reads=[r_mT, r_wout], writes=[preg[bo[half]]], sig=(k == 7))
            return bo

        def resid(t, bo):
            xi = t % 2
            mi = t % 2
            for half in range(2):
                S.op("dve", lambda half=half: dve.tensor_tensor(out=xm[mi][:, half * 512:(half + 1) * 512], in0=pbank[bo[half]][:, :],
                                                                in1=G12[:, half * 512:(half + 1) * 512], op=ALU.mult),
                     reads=[preg[bo[half]], r_g12b], writes=[r_xm[mi]])
            S.op("dve", lambda: dve.tensor_tensor(out=xm[mi], in0=xm[mi], in1=xts[xi], op=ALU.add),
                 reads=[r_xm[mi], r_xt[xi]], writes=[r_xm[mi]])
            S.dma("sp", "xmo%d" % mi, d_xmid[t * 128:(t + 1) * 128, :], xm[mi], reads=[r_xm[mi]], writes=[reg("dxm%d" % t)])
            return norm_A(r_xm[mi], xm[mi])

        bo_cur = wout_mm(0)
        hprev = None
        for t in range(16):
            if t + 1 < 16:
                S.dma("sp", "xt%d" % ((t + 1) % 2), xts[(t + 1) % 2], d_xall[(t + 1) * 128:(t + 2) * 128, :], writes=[r_xt[(t + 1) % 2]])
            bo_next = wout_mm(t + 1) if t + 1 < 16 else None
            hcur = resid(t, bo_cur)
            if hprev is not None:
                norm_B(hprev[0], lambda j: A2[:, j * 2:j * 2 + 1], lambda j: pm[:, 32 + j * 2:32 + j * 2 + 1],
                       lambda j, tp_=hprev[1]: hx2T[:, j * OWN + tp_ * 128:j * OWN + (tp_ + 1) * 128], r_hx2, force_act=True)
            hprev = (hcur, t)
            bo_cur = bo_next
        norm_B(hprev[0], lambda j: A2[:, j * 2:j * 2 + 1], lambda j: pm[:, 32 + j * 2:32 + j * 2 + 1],
               lambda j, tp_=hprev[1]: hx2T[:, j * OWN + tp_ * 128:j * OWN + (tp_ + 1) * 128], r_hx2, force_act=True)
        S.barrier()
        NAMES.update(hx2T=hx2T)
        checkpoint("p3iv")

        HALF = OWN // 2
        actT = bav(48, 22 * HALF)
        wfo = bav(92, 22 * D)
        wfi = [bav(136 + 4 * i, 8 * 256) for i in range(4)]
        r_actT, r_wfo = reg("actT"), reg("wfo")
        r_wfi = [reg("wfi%d" % i) for i in range(4)]
        S.dma("pool", "wfo", wfo.rearrange("p (k n) -> p k n", k=22), d_wfo[:, :].rearrange("(k p) n -> p k n", p=128), writes=[r_wfo])
        res = [FA[:, 2048:3072], FA[:, 3072:4096]]
        r_res = [reg("res0"), reg("res1")]
        wcnt = 0
        for hf in range(2):
            t0 = hf * HALF
            for jj in range(22):
                s = wcnt % 4
                wcnt += 1
                for ci, cb in enumerate((0, DFF)):
                    S.dma("pool", "wfi%d" % s, wfi[s][:, :].rearrange("p (k n) -> p k n", k=8)[:, :, ci * 128:(ci + 1) * 128],
                          d_wfi[:, cb + jj * 128:cb + (jj + 1) * 128].rearrange("(k p) n -> p k n", p=128), writes=[r_wfi[s]])
                for q in range(2):
                    c0 = t0 + q * 512
                    bg, bu = nextbank(), nextbank()
                    for (bk, off) in ((bg, 0), (bu, 128)):
                        for k in range(8):
                            S.op("pe", lambda k=k, bk=bk, off=off: pe.matmul(
                                pbank[bk][:, :], lhsT=wfi[s][:, k * 256 + off:k * 256 + off + 128],
                                rhs=hx2T[:, k * OWN + c0:k * OWN + c0 + 512], start=(k == 0), stop=(k == 7)),
                                reads=[r_wfi[s], r_hx2], writes=[preg[bk]], sig=(k == 7))
                    tsel = state.setdefault("tsel", 0)
                    state["tsel"] = tsel + 1
                    tX, rX = TMPS[tsel % 4]
                    S.op("act", lambda: act.activation(out=tX, in_=pbank[bg][:, :], func=AF.Silu), reads=[preg[bg]], writes=[rX])
                    S.op("dve", lambda: dve.tensor_tensor(out=actT[:, jj * HALF + q * 512:jj * HALF + (q + 1) * 512], in0=pbank[bu][:, :],
                                                          in1=tX, op=ALU.mult),
                         reads=[preg[bu], rX], writes=[r_actT])
            for tt in range(8):
                t = hf * 8 + tt
                xi = t % 2
                S.dma("sp", "xt%d" % xi, xts[xi], d_xmid[t * 128:(t + 1) * 128, :], reads=[reg("dxm%d" % t)], writes=[r_xt[xi]])
                bo = [nextbank(), nextbank()]
                for half in range(2):
                    for k in range(22):
                        S.op("pe", lambda k=k, half=half: pe.matmul(pbank[bo[half]][:, :],
                                                                    lhsT=actT[:, k * HALF + tt * 128:k * HALF + (tt + 1) * 128],
                                                                    rhs=wfo[:, k * D + half * 512:k * D + (half + 1) * 512],
                                                                    start=(k == 0), stop=(k == 21)),
                             reads=[r_actT, r_wfo], writes=[preg[bo[half]]], sig=(k == 21))
                ri = t % 2
                for half in range(2):
                    S.op("dve", lambda half=half: dve.tensor_tensor(out=res[ri][:, half * 512:(half + 1) * 512], in0=pbank[bo[half]][:, :],
                                                                    in1=G12[:, D + half * 512:D + (half + 1) * 512], op=ALU.mult),
                         reads=[preg[bo[half]], r_g12b], writes=[r_res[ri]])
                S.op("dve", lambda: dve.tensor_tensor(out=res[ri], in0=res[ri], in1=xts[xi], op=ALU.add),
                     reads=[r_res[ri], r_xt[xi]], writes=[r_res[ri]])
                S.dma("sp", "out%d" % ri, d_out[t * 128:(t + 1) * 128, :], res[ri], reads=[r_res[ri]], writes=[reg("dout%d" % t)], is_out=True)
        S.finish()


def _fm(v, nch):
    return np.ascontiguousarray(np.asarray(v, np.float32).reshape(nch, 128).T)


def _prep_shared(inp):
    f = lambda k: np.asarray(inp[k], np.float32)
    w_in = f("w_in")[0]
    w_qb = f("w_q_b")[0]
    base = np.zeros((128, NCONST), np.float32)
    base[:, C_NMIX:C_NMIX + 8] = _fm(f("norm_mix")[0], 8)
    base[:, C_NFFN:C_NFFN + 8] = _fm(f("norm_ffn")[0], 8)
    bm = f("b_mod")[0]
    for pi, ch in enumerate((0, 1, 3, 4)):
        base[:, C_BMOD + pi * 8:C_BMOD + (pi + 1) * 8] = _fm(bm[ch * D:(ch + 1) * D], 8)
    cw = f("conv_w")[0]
    for k in range(3):
        base[:, C_CONVW + k * 8:C_CONVW + (k + 1) * 8] = _fm(cw[k], 8)
    base[:, C_CONVB:C_CONVB + 8] = _fm(f("conv_b")[0], 8)
    base[:, C_BGATE:C_BGATE + 16] = _fm(f("b_gate")[0], 16)
    base[:, C_QAN:C_QAN + 3] = _fm(f("q_a_norm")[0], 3)
    base[:, C_KVAN:C_KVAN + 2] = _fm(f("kv_a_norm")[0], 2)
    qn, kn = f("q_norm")[0], f("k_norm")[0]
    base[:, C_QN_NOPE] = qn[:128]
    base[:64, C_QN_ROPE] = qn[128:]
    base[:64, C_QN_ROPESW] = qn[128:][ROPE_PERM]
    base[:, C_KN_NOPE] = kn[:128]
    base[:64, C_KN_ROPE] = kn[128:]
    base[:64, C_KN_ROPESW] = kn[128:][ROPE_PERM]
    p = np.arange(64)
    base[:64, C_FIDX] = (p % 16).astype(np.float32)
    base[:, C_SIGN] = 1.0
    base[:64, C_SIGN] = np.where((p % 32) < 16, -1.0, 1.0)
    base[32:64, C_POS:C_POS + 128] = np.arange(128, dtype=np.float32)[None, :]
    kr = w_in[:, 3712:3776]
    w_kr = np.concatenate([kr, kr, kr[:, ROPE_PERM], np.zeros_like(kr)], axis=1)
    wq = w_qb.reshape(384, NH, 192)
    z = np.zeros((384, NH, 64), np.float32)
    w_qb_ext = np.concatenate([wq[:, :, :128], wq[:, :, 128:], z, wq[:, :, 128:][:, :, ROPE_PERM], z], axis=2).reshape(384, NH * 384)
    shared = {
        "w_mod": np.ascontiguousarray(f("w_mod")[0]),
        "w_in": np.ascontiguousarray(w_in),
        "w_kr": np.ascontiguousarray(w_kr),
        "w_qb": np.ascontiguousarray(w_qb_ext),
        "w_kvb": np.ascontiguousarray(f("w_kv_b")[0]),
        "w_co": np.ascontiguousarray(f("w_conv_out")[0]),
        "w_ao": np.ascontiguousarray(f("w_attn_o")[0]),
        "w_out": np.ascontiguousarray(f("w_out")[0]),
        "w_fi": np.ascontiguousarray(f("w_ffn_in")[0]),
        "w_fo": np.ascontiguousarray(f("w_ffn_out")[0]),
        "g12b": np.ascontiguousarray(np.broadcast_to(np.concatenate([bm[2 * D:3 * D], bm[5 * D:6 * D]])[None, :], (128, 2 * D))),
    }
    return base, shared


_NC_CACHE = {}


def make_in_maps(inputs):
    x = np.asarray(inputs["x"], np.float32)
    c = np.asarray(inputs["c"], np.float32)
    ctx = np.asarray(inputs["ctx"], np.float32)
    c_ctx = np.asarray(inputs["c_ctx"], np.float32)
    base, shared = _prep_shared(inputs)
    in_maps = []
    for core in range(8):
        b, qc = core // 4, core % 4
        xall = np.concatenate([np.roll(x[b], -qc * OWN, axis=0), ctx[b]], axis=0)
        cons = base.copy()
        cons[:, C_MASK] = 1.0 if qc > 0 else 0.0
        cons[:, C_MASK + 1] = 1.0 if qc < 3 else 0.0
        cons[:32, C_POS:C_POS + 128] = ((np.arange(128) + 32 * qc) % 128).astype(np.float32)[None, :]
        cv = np.stack([c[b], c_ctx], axis=1)
        cT = np.ascontiguousarray(cv.reshape(8, 128, 2).transpose(1, 0, 2).reshape(128, 16))
        m = {"xall": np.ascontiguousarray(xall), "consts": cons, "cT": cT}
        m.update(shared)
        in_maps.append(m)
    return in_maps


def kernel(**inputs):
    in_maps = make_in_maps(inputs)
    if "nc" not in _NC_CACHE:
        _NC_CACHE["nc"] = build_nc()
    nc = _NC_CACHE["nc"]
    res = run_bass_kernel_spmd(nc, in_maps, core_ids=list(range(8)))
    out = np.empty((2, SEQ, D), np.float32)
    for core in range(8):
        b, qc = core // 4, core % 4
        out[b, qc * OWN:(qc + 1) * OWN] = res.results[core]["out"]
    return out
```
